# Optimizing a Trainium2 kernel written in Bass

```python
import jax
import jax.numpy as jnp
from jax import lax
import numpy as np

D_MODEL = 1024
BATCH = 8
SEQ = 4096
DEPTH = 2

GRID_W = 64
CTX_LEN = 256
HEAD_DIM = 64
BRANCH_WIDTH = 512
N_BRANCH = 3
SGU_GROUPS = 4
SGU_GROUP_DIM = BRANCH_WIDTH // SGU_GROUPS
SGU_CHUNK = 128
SWA_Q_HEADS = BRANCH_WIDTH // HEAD_DIM
SWA_KV_HEADS = 2
SWA_GROUP = SWA_Q_HEADS // SWA_KV_HEADS
SWA_WINDOW = 128
SWA_BLOCK = 128
ROPE_BASE = 10000.0
GLA_HEADS = 4
GLA_DK = 64
GLA_DV = BRANCH_WIDTH // GLA_HEADS
GLA_RANK = 16
GLA_NORMALIZER = 16.0
GLA_CHUNK = 64
FFN_DIM = 4 * D_MODEL
EPS = 1e-6

IN_SPLITS = (
    BRANCH_WIDTH, BRANCH_WIDTH,
    SWA_Q_HEADS * HEAD_DIM, SWA_KV_HEADS * HEAD_DIM, SWA_KV_HEADS * HEAD_DIM,
    GLA_HEADS * GLA_DK, GLA_HEADS * GLA_DK, BRANCH_WIDTH, BRANCH_WIDTH,
    GLA_RANK, GLA_RANK,
    N_BRANCH * D_MODEL,
)
IN_DIM = sum(IN_SPLITS)

kernel_name = 'hybrid_gated_dit_block'


def rms_norm(x, gain=None):
    xf = x.astype(jnp.float32)
    y = xf * lax.rsqrt(jnp.mean(xf * xf, axis=-1, keepdims=True) + EPS)
    if gain is not None:
        y = y * gain.astype(jnp.float32)
    return y.astype(x.dtype)


def adaln(cond, w_ada, b_ada):
    return jnp.split(jax.nn.silu(cond) @ w_ada + b_ada, 6, axis=-1)


def modulate(x, shift, scale):
    return rms_norm(x) * (1.0 + scale) + shift


def project(h, w_in):
    idx = np.cumsum(IN_SPLITS)[:-1].tolist()
    return jnp.split(h @ w_in, idx, axis=-1)


def heads(t, dim):
    return t.reshape(*t.shape[:-1], -1, dim)


def axial_rope(x, row, col):
    n_freq = HEAD_DIM // 4
    freqs = ROPE_BASE ** (-jnp.arange(n_freq, dtype=jnp.float32) / n_freq)

    def rotate(xa, pos):
        ang = pos.astype(jnp.float32)[:, None] * freqs[None, :]
        cos = jnp.cos(ang)[None, :, None, :]
        sin = jnp.sin(ang)[None, :, None, :]
        x1, x2 = xa[..., :n_freq], xa[..., n_freq:]
        return jnp.concatenate([x1 * cos - x2 * sin, x1 * sin + x2 * cos], axis=-1)

    xf = x.astype(jnp.float32)
    half = HEAD_DIM // 2
    out = jnp.concatenate([rotate(xf[..., :half], row), rotate(xf[..., half:], col)], axis=-1)
    return out.astype(x.dtype)


def sgu_mix(u, v, gain, w_s, b_s):
    B, L, _ = u.shape
    nc = L // SGU_CHUNK
    u = jax.nn.gelu(u)
    v = rms_norm(jax.nn.gelu(v).reshape(B, L, SGU_GROUPS, SGU_GROUP_DIM),
                 gain.reshape(SGU_GROUPS, SGU_GROUP_DIM))
    v = v.reshape(B, nc, SGU_CHUNK, SGU_GROUPS, SGU_GROUP_DIM)
    s = jnp.einsum('gpq,bcqgd->bcpgd', w_s, v) + b_s.T[None, None, :, :, None]
    return u * s.reshape(B, L, BRANCH_WIDTH)


def swa_latent(q, k, v, k_ctx, v_ctx, sink):
    B, N = q.shape[0], q.shape[1]
    n_blk = N // SWA_BLOCK
    pad = ((0, 0), (SWA_BLOCK, SWA_BLOCK), (0, 0), (0, 0))
    k_pad = jnp.pad(k, pad)
    v_pad = jnp.pad(v, pad)
    scale = HEAD_DIM ** -0.5
    sink_l = jnp.broadcast_to(sink.reshape(SWA_KV_HEADS, SWA_GROUP).astype(jnp.float32)[None, :, :, None, None],
                              (B, SWA_KV_HEADS, SWA_GROUP, SWA_BLOCK, 1))
    rel = (jnp.arange(3 * SWA_BLOCK) - SWA_BLOCK)[None, :] - jnp.arange(SWA_BLOCK)[:, None]
    n_loc = 3 * SWA_BLOCK

    def one_block(i):
        start = i * SWA_BLOCK
        qb = lax.dynamic_slice_in_dim(q, start, SWA_BLOCK, axis=1)
        kb = lax.dynamic_slice_in_dim(k_pad, start, n_loc, axis=1)
        vb = lax.dynamic_slice_in_dim(v_pad, start, n_loc, axis=1)
        kpos = start - SWA_BLOCK + jnp.arange(n_loc)
        valid = (jnp.abs(rel) <= SWA_WINDOW) & ((kpos >= 0) & (kpos < N))[None, :]
        s_loc = jnp.einsum('bqhgd,bkhd->bhgqk', qb, kb).astype(jnp.float32) * scale
        s_loc = jnp.where(valid, s_loc, -jnp.inf)
        s_ctx = jnp.einsum('bqhgd,bkhd->bhgqk', qb, k_ctx).astype(jnp.float32) * scale
        p = jax.nn.softmax(jnp.concatenate([s_loc, s_ctx, sink_l], axis=-1), axis=-1).astype(v.dtype)
        o = jnp.einsum('bhgqk,bkhd->bqhgd', p[..., :n_loc], vb)
        o = o + jnp.einsum('bhgqk,bkhd->bqhgd', p[..., n_loc:n_loc + k_ctx.shape[1]], v_ctx)
        return o

    out = lax.map(one_block, jnp.arange(n_blk))
    return jnp.moveaxis(out, 0, 1).reshape(B, N, BRANCH_WIDTH)


def ctx_attention(q, k, v, sink):
    B, C = q.shape[0], q.shape[1]
    s = jnp.einsum('bqhgd,bkhd->bhgqk', q, k).astype(jnp.float32) * HEAD_DIM ** -0.5
    s_sink = jnp.broadcast_to(sink.reshape(SWA_KV_HEADS, SWA_GROUP).astype(jnp.float32)[None, :, :, None, None],
                              (B, SWA_KV_HEADS, SWA_GROUP, C, 1))
    p = jax.nn.softmax(jnp.concatenate([s, s_sink], axis=-1), axis=-1)[..., :-1].astype(v.dtype)
    return jnp.einsum('bhgqk,bkhd->bqhgd', p, v).reshape(B, C, BRANCH_WIDTH)


def gla_log_decay(z, w2, b2):
    logit = (z @ w2 + b2).astype(jnp.float32)
    return (jax.nn.log_sigmoid(logit) / GLA_NORMALIZER).reshape(*z.shape[:-1], GLA_HEADS, GLA_DK)


def gla_chunked(q, k, v, g, s0):
    B, L, H, DK = q.shape
    DV = v.shape[-1]
    nc = L // GLA_CHUNK

    def chunks(t):
        return t.astype(jnp.float32).reshape(B, nc, GLA_CHUNK, H, t.shape[-1])

    q = chunks(q) * GLA_DK ** -0.5
    k, v, g = chunks(k), chunks(v), chunks(g)
    b = jnp.cumsum(g, axis=2)
    b_ref = b[:, :, GLA_CHUNK // 2][:, :, None]
    b_tot = b[:, :, -1]
    scores = jnp.einsum('bclhd,bcmhd->bchlm', q * jnp.exp(b - b_ref), k * jnp.exp(b_ref - b))
    lower = jnp.tril(jnp.ones((GLA_CHUNK, GLA_CHUNK), dtype=bool))
    scores = jnp.where(lower, scores, 0.0)
    o_intra = jnp.einsum('bchlm,bcmhv->bclhv', scores, v)
    q_in = q * jnp.exp(b)
    k_out = k * jnp.exp(b_tot[:, :, None] - b)

    def step(state, inp):
        q_c, k_c, v_c, bt = inp
        o_c = jnp.einsum('blhd,bhdv->blhv', q_c, state)
        state = state * jnp.exp(bt)[..., None] + jnp.einsum('blhd,blhv->bhdv', k_c, v_c)
        return state, o_c

    xs = (jnp.moveaxis(q_in, 1, 0), jnp.moveaxis(k_out, 1, 0), jnp.moveaxis(v, 1, 0), jnp.moveaxis(b_tot, 1, 0))
    s_fin, o_inter = lax.scan(step, s0.astype(jnp.float32), xs)
    o = o_intra + jnp.moveaxis(o_inter, 0, 1)
    return o.reshape(B, L, H, DV), s_fin


def gla_bidir(q, k, v, g_f, g_b, s_f, s_b):
    o_f, sf = gla_chunked(q, k, v, g_f, s_f)
    o_b, sb = gla_chunked(jnp.flip(q, 1), jnp.flip(k, 1), jnp.flip(v, 1), jnp.flip(g_b, 1), s_b)
    return o_f + jnp.flip(o_b, 1), sf, sb


def gla_out(o, r, gain):
    return rms_norm(o, gain).astype(r.dtype).reshape(r.shape) * jax.nn.silu(r)


def merge_branches(a, b, cc, gates, w_br, w_o):
    br = jnp.stack([a, b, cc], axis=-2)
    proj = jnp.einsum('blkw,kwd->blkd', br, w_br)
    gate = jax.nn.sigmoid(gates.reshape(*gates.shape[:-1], N_BRANCH, D_MODEL))
    return jnp.sum(gate * proj, axis=-2) @ w_o


def sq_relu_ffn(h, w1, w2):
    return jnp.square(jax.nn.relu(h @ w1)) @ w2


def setup_inputs(seed: int = 0) -> dict:
    key = jax.random.key(seed)
    ks = jax.random.split(key, 24)

    def nrm(k, shape, scale):
        return jax.random.normal(k, shape, jnp.float32) * scale

    def gain(k, shape):
        return 1.0 + 0.02 * jax.random.normal(k, shape, jnp.float32)

    return {
        'x': nrm(ks[0], (BATCH, SEQ, D_MODEL), 1.0),
        'c': nrm(ks[1], (BATCH, D_MODEL), 1.0),
        'ctx': nrm(ks[2], (BATCH, CTX_LEN, D_MODEL), 1.0),
        'c_ctx': nrm(ks[3], (D_MODEL,), 1.0),
        'w_ada': nrm(ks[4], (DEPTH, D_MODEL, 6 * D_MODEL), D_MODEL ** -0.5),
        'b_ada': nrm(ks[5], (DEPTH, 6 * D_MODEL), 0.01),
        'w_in': nrm(ks[6], (DEPTH, D_MODEL, IN_DIM), D_MODEL ** -0.5),
        'q_norm': gain(ks[7], (DEPTH, HEAD_DIM)),
        'k_norm': gain(ks[8], (DEPTH, HEAD_DIM)),
        'sink': nrm(ks[9], (DEPTH, SWA_Q_HEADS), 1.0),
        'sgu_norm': gain(ks[10], (DEPTH, BRANCH_WIDTH)),
        'w_sgu': nrm(ks[11], (DEPTH, SGU_GROUPS, SGU_CHUNK, SGU_CHUNK), SGU_CHUNK ** -0.5),
        'b_sgu': 1.0 + nrm(ks[12], (DEPTH, SGU_GROUPS, SGU_CHUNK), 0.01),
        'w_gate_f': nrm(ks[13], (DEPTH, GLA_RANK, GLA_HEADS * GLA_DK), GLA_RANK ** -0.5),
        'b_gate_f': nrm(ks[14], (DEPTH, GLA_HEADS * GLA_DK), 0.01),
        'w_gate_b': nrm(ks[15], (DEPTH, GLA_RANK, GLA_HEADS * GLA_DK), GLA_RANK ** -0.5),
        'b_gate_b': nrm(ks[16], (DEPTH, GLA_HEADS * GLA_DK), 0.01),
        'gla_norm': gain(ks[17], (DEPTH, GLA_DV)),
        'w_br': nrm(ks[18], (DEPTH, N_BRANCH, BRANCH_WIDTH, D_MODEL), BRANCH_WIDTH ** -0.5),
        'w_o': nrm(ks[19], (DEPTH, D_MODEL, D_MODEL), D_MODEL ** -0.5),
        'w_ff1': nrm(ks[20], (DEPTH, D_MODEL, FFN_DIM), D_MODEL ** -0.5),
        'w_ff2': nrm(ks[21], (DEPTH, FFN_DIM, D_MODEL), FFN_DIM ** -0.5),
    }


def reference(x, c, ctx, c_ctx, w_ada, b_ada, w_in, q_norm, k_norm, sink, sgu_norm, w_sgu, b_sgu,
              w_gate_f, b_gate_f, w_gate_b, b_gate_b, gla_norm, w_br, w_o, w_ff1, w_ff2):
    B, N, _ = x.shape
    C = ctx.shape[1]
    rows = N // GRID_W
    row = jnp.repeat(jnp.arange(rows), GRID_W)
    col = jnp.tile(jnp.arange(GRID_W), rows)
    zero_state = jnp.zeros((B, GLA_HEADS, GLA_DK, GLA_DV), jnp.float32)

    for l in range(DEPTH):
        sh1, sc1, g1, sh2, sc2, g2 = [m[:, None, :] for m in adaln(c, w_ada[l], b_ada[l])]
        csh1, csc1, cg1, csh2, csc2, cg2 = adaln(c_ctx, w_ada[l], b_ada[l])

        (cu, cva, cqb, ckb, cvb, cqc, ckc, cvc, crc, czf, czb, cgates) = project(modulate(ctx, csh1, csc1), w_in[l])
        ck_att = rms_norm(heads(ckb, HEAD_DIM), k_norm[l])
        cv_att = heads(cvb, HEAD_DIM)
        oc, s_fwd, s_bwd = gla_bidir(heads(cqc, GLA_DK), heads(ckc, GLA_DK), heads(cvc, GLA_DV),
                                     gla_log_decay(czf, w_gate_f[l], b_gate_f[l]),
                                     gla_log_decay(czb, w_gate_b[l], b_gate_b[l]),
                                     zero_state, zero_state)

        (u, va, qb, kb, vb, qc, kc, vc, rc, zf, zb, gates) = project(modulate(x, sh1, sc1), w_in[l])
        a_out = sgu_mix(u, va, sgu_norm[l], w_sgu[l], b_sgu[l])
        q_att = axial_rope(rms_norm(heads(qb, HEAD_DIM), q_norm[l]), row, col)
        q_att = q_att.reshape(B, N, SWA_KV_HEADS, SWA_GROUP, HEAD_DIM)
        k_att = axial_rope(rms_norm(heads(kb, HEAD_DIM), k_norm[l]), row, col)
        b_out = swa_latent(q_att, k_att, heads(vb, HEAD_DIM), ck_att, cv_att, sink[l])
        o_lat, _, _ = gla_bidir(heads(qc, GLA_DK), heads(kc, GLA_DK), heads(vc, GLA_DV),
                                gla_log_decay(zf, w_gate_f[l], b_gate_f[l]),
                                gla_log_decay(zb, w_gate_b[l], b_gate_b[l]),
                                s_fwd, s_bwd)
        c_out = gla_out(o_lat, rc, gla_norm[l])
        x_new = x + g1 * merge_branches(a_out, b_out, c_out, gates, w_br[l], w_o[l])
        x_new = x_new + g2 * sq_relu_ffn(modulate(x_new, sh2, sc2), w_ff1[l], w_ff2[l])

        if l < DEPTH - 1:
            ca_out = sgu_mix(cu, cva, sgu_norm[l], w_sgu[l], b_sgu[l])
            cq_att = rms_norm(heads(cqb, HEAD_DIM), q_norm[l]).reshape(B, C, SWA_KV_HEADS, SWA_GROUP, HEAD_DIM)
            cb_out = ctx_attention(cq_att, ck_att, cv_att, sink[l])
            cc_out = gla_out(oc, crc, gla_norm[l])
            ctx = ctx + cg1 * merge_branches(ca_out, cb_out, cc_out, cgates, w_br[l], w_o[l])
            ctx = ctx + cg2 * sq_relu_ffn(modulate(ctx, csh2, csc2), w_ff1[l], w_ff2[l])
        x = x_new

    return x
```

```python
import contextlib
import itertools
import numpy as np
import concourse.bass as bass
import concourse.mybir as mybir
from concourse.bass_utils import run_bass_kernel_spmd

F32 = mybir.dt.float32
BF16 = mybir.dt.bfloat16
F32R = mybir.dt.float32r
AF = mybir.ActivationFunctionType
ALU = mybir.AluOpType
AX = mybir.AxisListType

NTOK = 4352
NLAT = 4096
NCTX = 256
D = 1024
DEPTH = 2
EPS = 1e-6
TG = [(0, 256)] + [(256 + 512 * i, 512) for i in range(8)]
TG256 = [(256 * i, 256) for i in range(17)]
GI = {t0: i for i, (t0, _) in enumerate(TG)}


class Region:
    __slots__ = ("name", "last_write", "reads")

    def __init__(self, name=""):
        self.name = name
        self.last_write = None
        self.reads = []


class _Eng:
    def __init__(self, key):
        self.key = key
        self.prog = []
        self.count = 0
        self.seen = {}


class Sched:
    NRING = 28

    def __init__(self, nc, stack):
        self.nc = nc
        self.engs = {k: _Eng(k) for k in ("pe", "dve", "act", "pool", "sp")}
        self.ring_n = [0] * self.NRING
        self.ring_next = 0
        self.sems = {}
        for k in self.engs:
            self.sems[k] = stack.enter_context(nc.semaphore("s_" + k))
        for j in range(self.NRING):
            self.sems["ring%d" % j] = stack.enter_context(nc.semaphore("s_ring%d" % j))

    def _deps(self, reads, writes):
        deps = []
        for r in reads:
            if r.last_write is not None:
                deps.append(r.last_write)
        for w in writes:
            if w.last_write is not None:
                deps.append(w.last_write)
            deps.extend(w.reads)
        return deps

    def _emit_waits(self, eng, deps):
        best = {}
        for (k, v) in deps:
            if k == "pe" and eng.key == "pe":
                continue
            if best.get(k, 0) < v:
                best[k] = v
        for k, v in best.items():
            if eng.seen.get(k, 0) >= v:
                continue
            eng.seen[k] = v
            eng.prog.append(("wait", k, v))

    def _record(self, tok, reads, writes):
        for r in reads:
            r.reads.append(tok)
        for w in writes:
            w.last_write = tok
            w.reads = []

    def op(self, ekey, fn, reads=(), writes=()):
        eng = self.engs[ekey]
        self._emit_waits(eng, self._deps(reads, writes))
        eng.count += 1
        tok = (ekey, eng.count)
        eng.prog.append(("op", fn, ekey))
        self._record(tok, reads, writes)
        return tok

    def dma(self, qkey, fn, reads=(), writes=()):
        eng = self.engs[qkey]
        j = self.ring_next
        self.ring_next = (self.ring_next + 1) % self.NRING
        rk = "ring%d" % j
        deps = self._deps(reads, writes)
        if self.ring_n[j] > 0:
            deps.append((rk, 16 * self.ring_n[j]))
        self._emit_waits(eng, deps)
        self.ring_n[j] += 1
        tok = (rk, 16 * self.ring_n[j])
        eng.prog.append(("dma", fn, rk))
        self._record(tok, reads, writes)
        return tok

    def barrier(self):
        toks = [(k, e.count) for k, e in self.engs.items() if e.count > 0]
        toks += [("ring%d" % j, 16 * n) for j, n in enumerate(self.ring_n) if n > 0]
        for e in self.engs.values():
            self._emit_waits(e, toks)

    def flush(self):
        self.barrier()
        nc = self.nc
        sems = self.sems
        with nc.Block() as block:
            def run(eng):
                def body(h):
                    for item in eng.prog:
                        if item[0] == "wait":
                            h.wait_ge(sems[item[1]], item[2])
                        elif item[0] == "op":
                            item[1](h).then_inc(sems[item[2]], 1)
                        else:
                            item[1](h).then_inc(sems[item[2]], 16)
                return body
            block.tensor(run(self.engs["pe"]))
            block.vector(run(self.engs["dve"]))
            block.scalar(run(self.engs["act"]))
            block.gpsimd(run(self.engs["pool"]))
            block.sync(run(self.engs["sp"]))
        for e in self.engs.values():
            e.prog = []


class Rot:
    uid = 0

    def __init__(self, nc, st, name, shape, dt, n=2):
        Rot.uid += 1
        self.bufs = [st.enter_context(nc.sbuf_tensor("rt%d_%s_%d" % (Rot.uid, name, i), list(shape), dt)) for i in range(n)]
        self.regs = [Region("%s_%d" % (name, i)) for i in range(n)]
        self.subregs = [[Region("%s_%d_%d" % (name, i, j)) for j in range(8)] for i in range(n)]
        self.i = 0

    def next2(self):
        i = self.i
        b, r = self.next()
        return b, r, self.subregs[i]

    def next(self):
        b, r = self.bufs[self.i], self.regs[self.i]
        self.i = (self.i + 1) % len(self.bufs)
        return b, r


def build(debug=False, nlayers=DEPTH):
    nc = bass.Bass("TRN2", target_bir_lowering=False)

    def din(name, shape, dt=F32):
        return nc.dram_tensor(name, list(shape), dt, kind="ExternalInput").ap()

    def dscr(name, shape, dt, dbg=False):
        if dbg and debug:
            return nc.dram_tensor(name, list(shape), dt, kind="ExternalOutput").ap()
        return nc.dram_tensor(name, list(shape), dt).ap()

    xcat = din("xcat", [NTOK, D])
    cmod_d = din("cmod", [128, 8, 2])
    w_ada = din("w_ada", [DEPTH, D, 6 * D])
    badaT_d = din("badaT", [DEPTH, 128, 48])
    badaR_d = din("badaR", [DEPTH, 2, 6 * D])
    w_in = din("w_in", [DEPTH, D, 6432])
    qg_d = din("qg64", [DEPTH, 128, 2, 64])
    kg_d = din("kg64", [DEPTH, 128, 2, 64])
    sink_d = din("sinkb", [DEPTH, 128, 8])
    sgcol_d = din("sgcol", [DEPTH, 128, 4])
    wsguT_d = din("wsguT", [DEPTH, 128, 4, 128])
    bsgu_d = din("bsgub", [DEPTH, 128, 4, 128])
    wgate_d = din("wgate", [DEPTH, 2, 16, 256])
    bgrow_d = din("bgrow", [DEPTH, 1, 2, 256])
    glcol_d = din("glcol", [DEPTH, 128, 1])
    w_br = din("w_br", [DEPTH, 3, 512, D])
    w_o = din("w_o", [DEPTH, D, D])
    w_ff1 = din("w_ff1", [DEPTH, D, 4096])
    w_ff2 = din("w_ff2", [DEPTH, 4096, D])
    ropeC_d = din("ropeC", [128, 34, 64])
    ropeS_d = din("ropeS", [128, 34, 64])
    ident_d = din("ident", [128, 128])
    mge_d = din("mask_ge", [128, 128])
    mle_d = din("mask_le", [128, 128])
    sel_d = din("sel", [2, 2, 128])
    out = nc.dram_tensor("out", [NLAT, D], F32, kind="ExternalOutput").ap()

    brT = dscr("brT", [9, 128, 12, 512], BF16, True)
    sig = dscr("sig", [9, 128, 24, 512], BF16, True)
    qT_s = dscr("qT_s", [512, NTOK], BF16, True)
    kT_s = dscr("kT_s", [128, NTOK], BF16, True)
    Vx_s = dscr("Vx_s", [NTOK, 130], BF16, True)
    vn_s = dscr("vn_s", [NTOK, 512], BF16, True)
    qkc_s = dscr("qkc_s", [512, NTOK], F32, True)
    vc_s = dscr("vc_s", [NTOK, 512], BF16, True)
    rs_s = dscr("rs_s", [NTOK, 512], BF16, True)
    zT_s = dscr("zT_s", [32, NTOK], F32, True)
    of_s = dscr("of_s", [NTOK, 512], F32, True)
    ob_s = dscr("ob_s", [NTOK, 512], F32, True)
    xm = dscr("xm", [NTOK, D], F32, True)
    xa = dscr("xa", [NTOK, D], F32, True)

    top = contextlib.ExitStack()
    with top:
        S = Sched(nc, top)

        def MM(out_, lhsT, rhs, start, stop, r, w):
            S.op("pe", lambda h: h.matmul(out_, lhsT=lhsT, rhs=rhs, start=start, stop=stop), r, w)

        def TR(out_, in_, ident, r, w):
            S.op("pe", lambda h: h.transpose(out=out_, in_=in_, identity=ident), r, w)

        def ACTF(out_, in_, func, r, w, scale=None, bias=None, accum=None):
            kw = {}
            if scale is not None:
                kw["scale"] = scale
            if bias is not None:
                kw["bias"] = bias
            if accum is not None:
                kw["accum_out"] = accum
            S.op("act", lambda h: h.activation(out=out_, in_=in_, func=func, **kw), r, w)

        def TT(eng, out_, in0, in1, op, r, w):
            S.op(eng, lambda h: h.tensor_tensor(out=out_, in0=in0, in1=in1, op=op), r, w)

        def TS(eng, out_, in0, s1, op0, r, w, s2=None, op1=None):
            if op1 is None:
                S.op(eng, lambda h: h.tensor_scalar(out=out_, in0=in0, scalar1=s1, scalar2=None, op0=op0), r, w)
            else:
                S.op(eng, lambda h: h.tensor_scalar(out=out_, in0=in0, scalar1=s1, scalar2=s2, op0=op0, op1=op1), r, w)

        def STT(out_, in0, scalar, in1, op0, op1, r, w):
            S.op("dve", lambda h: h.scalar_tensor_tensor(out=out_, in0=in0, scalar=scalar, in1=in1, op0=op0, op1=op1), r, w)

        def CP(eng, out_, in_, r, w):
            if eng == "act":
                ACTF(out_, in_, AF.Copy, r, w)
            else:
                S.op(eng, lambda h: h.tensor_copy(out=out_, in_=in_), r, w)

        def RED(out_, in_, r, w):
            S.op("dve", lambda h: h.tensor_reduce(out=out_, in_=in_, axis=AX.X, op=ALU.add), r, w)

        def RCP(out_, in_, r, w):
            S.op("dve", lambda h: h.reciprocal(out=out_, in_=in_), r, w)

        def MSET(eng, ap, val, w):
            S.op(eng, lambda h: h.memset(ap, val), (), w)

        def DMA(q, out_, in_, r, w):
            S.dma(q, lambda h: h.dma_start(out=out_, in_=in_), r, w)

        def sb(st, name, shape, dt):
            Rot.uid += 1
            t = st.enter_context(nc.sbuf_tensor("sb%d_%s" % (Rot.uid, name), list(shape), dt))
            return t, Region(name)

        pb = [top.enter_context(nc.psum_tensor("pb%d" % i, [128, 512], F32)) for i in range(8)]
        pbr = [Region("pb%d" % i) for i in range(8)]
        bank_ctr = [0]

        BKS = {"all": list(range(8)), "mm": [0, 1, 2, 3], "aux": [4, 5], "tr": [6, 7]}
        bctr = {k: 0 for k in BKS}

        def nbank(kind="all"):
            lst = BKS[kind]
            i = lst[bctr[kind] % len(lst)]
            bctr[kind] += 1
            return pb[i], pbr[i]

        def pipe(gens, depth):
            active = []
            it = iter(gens)
            done = False
            while True:
                if not done and len(active) < depth:
                    try:
                        active.append(next(it))
                    except StopIteration:
                        done = True
                if not active:
                    if done:
                        break
                    continue
                for g_ in list(active):
                    try:
                        next(g_)
                    except StopIteration:
                        active.remove(g_)

        identf, r_identf = sb(top, "identf", [128, 128], F32)
        identb, r_identb = sb(top, "identb", [128, 128], BF16)
        mge, r_mge = sb(top, "mge", [128, 128], BF16)
        mle, r_mle = sb(top, "mle", [128, 128], BF16)
        sel, r_sel = sb(top, "sel", [2, 2, 128], F32)
        cmod, r_cmod = sb(top, "cmod", [128, 8, 2], F32)
        scb, r_scb = sb(top, "scb", [128, 8, 2], F32)
        scbb, r_scbb = sb(top, "scbb", [128, 8, 2], BF16)
        modT, r_modT = sb(top, "modT", [128, 48, 2], F32)
        gbc, r_gbc = sb(top, "gbc", [128, 2, 2, 1024], F32)
        r_const = Region("const")
        negh, r_negh = sb(top, "negh", [128, 8], F32)
        MSET("dve", negh[:], -0.5, [r_negh])

        DMA("sp", identf[:], ident_d, [], [r_identf])
        DMA("sp", sel[:], sel_d, [], [r_sel])
        DMA("sp", cmod[:], cmod_d, [], [r_cmod])
        DMA("pool", mge[:], mge_d, [], [r_mge])
        DMA("pool", mle[:], mle_d, [], [r_mle])
        CP("dve", identb[:], identf[:], [r_identf], [r_identb])
        ACTF(scb[:], cmod[:], AF.Silu, [r_cmod], [r_scb])
        CP("dve", scbb[:], scb[:], [r_scb], [r_scbb])
        S.flush()

        def frontA(st_bufs, src, tok0, ntok, xt, r_xt):
            junk, r_junk, ss, r_ss, xn, r_xn = st_bufs
            if r_junk is None:
                r_junk = Region("junk")
            nsub = ntok // 128
            DMA("sp", xt[:, 0:nsub, :], src[tok0:tok0 + ntok, :].rearrange("(s p) d -> p s d", p=128), [], [r_xt])
            for sub in range(nsub):
                ACTF(junk[:], xt[:, sub, :], AF.Square, [r_xt], [r_junk, r_ss], accum=ss[:, sub:sub + 1])
            TS("pool", ss[:, 4:4 + nsub], ss[:, 0:nsub], 1.0 / D, ALU.mult, [r_ss], [r_ss], s2=EPS, op1=ALU.add)
            TT("pool", ss[:, 8:8 + nsub], ss[:, 4:4 + nsub], negh[:, 0:nsub], ALU.pow, [r_negh], [r_ss])
            for sub in range(nsub):
                if sub % 2 == 0:
                    TS("dve", xn[:, sub, :], xt[:, sub, :], ss[:, 8 + sub:9 + sub], ALU.mult, [r_xt, r_ss], [r_xn])
                else:
                    ACTF(xn[:, sub, :], xt[:, sub, :], AF.Identity, [r_xt, r_ss], [r_xn], scale=ss[:, 8 + sub:9 + sub])

        def frontB(st_bufs, ntok, sh_j, sc_j, s, dst, r_dst):
            junk, r_junk, ss, r_ss, xn, r_xn = st_bufs
            nsub = ntok // 128
            for kc in range(8):
                bk, rb = nbank("tr")
                bv = bk[:].bitcast(BF16)
                for sub in range(nsub):
                    TR(bv[:, sub * 128:(sub + 1) * 128], xn[:, sub, kc * 128:(kc + 1) * 128], identb[:],
                       [r_xn, r_identb], [rb])
                if kc % 2 == 0:
                    TS("dve", dst[:, kc, 0:ntok], bv[:, 0:ntok], modT[:, sc_j + kc, s:s + 1], ALU.mult,
                       [r_modT], [rb, r_dst], s2=modT[:, sh_j + kc, s:s + 1], op1=ALU.add)
                else:
                    ACTF(dst[:, kc, 0:ntok], bv[:, 0:ntok], AF.Identity, [r_modT], [rb, r_dst],
                         scale=modT[:, sc_j + kc, s:s + 1], bias=modT[:, sh_j + kc, s:s + 1])

        def grp_norm(wk, src, r_src, H, Dd, gain, r_gain, out_, r_out):
            sq, r_sq = wk["sq"].next()
            st4, r_st4 = wk["st"].next()
            tt, r_tt = wk["t"].next()
            W = H * Dd
            TT("pool", sq[:, 0:W], src, src, ALU.mult, [r_src], [r_sq])
            RED(st4[:, 0:H], sq[:, 0:W].rearrange("p (h d) -> p h d", d=Dd), [r_sq], [r_st4])
            ACTF(st4[:, 8:8 + H], st4[:, 0:H], AF.Sqrt, [r_st4], [r_st4], scale=1.0 / Dd, bias=EPS)
            RCP(st4[:, 16:16 + H], st4[:, 8:8 + H], [r_st4], [r_st4])
            TT("dve", tt[:, 0:W].rearrange("p (h d) -> p h d", d=Dd), src.rearrange("p (h d) -> p h d", d=Dd),
               st4[:, 16:16 + H].unsqueeze(2).to_broadcast([128, H, Dd]), ALU.mult, [r_src, r_st4], [r_tt])
            TT("pool", out_, tt[:, 0:W], gain, ALU.mult, [r_tt, r_gain], [r_out])

        def grp_norm_g(wk, src, r_src, H, Dd, gain, r_gain, out_, r_out):
            sq, r_sq = wk["sq"].next()
            st4, r_st4 = wk["st"].next()
            tt, r_tt = wk["t"].next()
            W = H * Dd
            TT("pool", sq[:, 0:W], src, src, ALU.mult, [r_src], [r_sq])
            yield
            RED(st4[:, 0:H], sq[:, 0:W].rearrange("p (h d) -> p h d", d=Dd), [r_sq], [r_st4])
            yield
            ACTF(st4[:, 8:8 + H], st4[:, 0:H], AF.Sqrt, [r_st4], [r_st4], scale=1.0 / Dd, bias=EPS)
            yield
            RCP(st4[:, 16:16 + H], st4[:, 8:8 + H], [r_st4], [r_st4])
            yield
            TT("dve", tt[:, 0:W].rearrange("p (h d) -> p h d", d=Dd), src.rearrange("p (h d) -> p h d", d=Dd),
               st4[:, 16:16 + H].unsqueeze(2).to_broadcast([128, H, Dd]), ALU.mult, [r_src, r_st4], [r_tt])
            yield
            TT("pool", out_, tt[:, 0:W], gain, ALU.mult, [r_tt, r_gain], [r_out])

        def rstd_g(wk, src, r_src, H, Dd, use_act, pool_pow=False):
            st4, r_st4 = wk["st"].next()
            W = H * Dd
            sq, r_sq = wk["sq"].next()
            if use_act:
                for h_ in range(H):
                    ACTF(sq[:, h_ * Dd:(h_ + 1) * Dd], src[:, h_ * Dd:(h_ + 1) * Dd], AF.Square, [r_src], [r_sq, r_st4],
                         accum=st4[:, h_:h_ + 1])
                yield None
            else:
                TT("pool", sq[:, 0:W], src, src, ALU.mult, [r_src], [r_sq])
                yield None
                RED(st4[:, 0:H], sq[:, 0:W].rearrange("p (h d) -> p h d", d=Dd), [r_sq], [r_st4])
                yield None
            if pool_pow:
                TS("pool", st4[:, 8:8 + H], st4[:, 0:H], 1.0 / Dd, ALU.mult, [r_st4], [r_st4], s2=EPS, op1=ALU.add)
                yield None
                TT("pool", st4[:, 16:16 + H], st4[:, 8:8 + H], negh[:, 0:H], ALU.pow, [r_negh], [r_st4])
            else:
                ACTF(st4[:, 8:8 + H], st4[:, 0:H], AF.Sqrt, [r_st4], [r_st4], scale=1.0 / Dd, bias=EPS)
                yield None
                RCP(st4[:, 16:16 + H], st4[:, 8:8 + H], [r_st4], [r_st4])
            yield (st4, r_st4)

        def norm_wk(st, pfx, n=2):
            return {"sq": Rot(nc, st, pfx + "sq", [128, 512], F32, n=n), "st": Rot(nc, st, pfx + "st", [128, 24], F32, n=n)}

        for l in range(nlayers):
            last = (l == DEPTH - 1)
            src_x = xcat if l == 0 else xa

            with contextlib.ExitStack() as st:
                wa = Rot(nc, st, "wa", [128, 8, 1024], F32, n=3)
                badaT, r_badaT = sb(st, "badaT", [128, 48], F32)
                badaR, r_badaR = sb(st, "badaR", [2, 6 * D], F32)
                grow = Rot(nc, st, "grow", [2, 512], F32, n=3)
                DMA("sp", badaT[:], badaT_d[l], [], [r_badaT])
                DMA("sp", badaR[:], badaR_d[l], [], [r_badaR])
                wabs = Rot(nc, st, "wab", [128, 8, 512], BF16, n=3)
                cast_eng = ["pool", "dve", "act"]
                for cg in range(12):
                    if cg % 2 == 0:
                        wt32, rw32 = wa.next()
                        DMA("sp", wt32[:], w_ada[l][:, cg * 512:(cg + 2) * 512].rearrange("(kc p) n -> p kc n", p=128),
                            [], [rw32])
                    wt, rw = wabs.next()
                    CP(cast_eng[cg % 3], wt[:], wt32[:, :, (cg % 2) * 512:(cg % 2 + 1) * 512], [rw32], [rw])
                    which = cg // 2
                    if which in (2, 5):
                        bk, rb = nbank()
                        for kc in range(8):
                            MM(bk[0:2, :], scbb[:, kc, :], wt[:, kc, :], kc == 0, kc == 7, [r_scbb, rw], [rb])
                        gr, rg = grow.next()
                        TT("dve", gr[:], bk[0:2, :], badaR[:, cg * 512:(cg + 1) * 512], ALU.add, [r_badaR], [rb, rg])
                        for s in range(2):
                            bk2, rb2 = nbank()
                            MM(bk2[:], sel[:, s, :], gr[:], True, True, [r_sel, rg], [rb2])
                            CP("act", gbc[:, 0 if which == 2 else 1, s, (cg % 2) * 512:(cg % 2 + 1) * 512], bk2[:],
                               [], [rb2, r_gbc])
                    else:
                        bk, rb = nbank()
                        for kc in range(8):
                            MM(bk[0:2, :], scbb[:, kc, :], wt[:, kc, :], kc == 0, kc == 7, [r_scbb, rw], [rb])
                        gr, rg = grow.next()
                        TT("dve", gr[:], bk[0:2, :], badaR[:, cg * 512:(cg + 1) * 512], ALU.add, [r_badaR], [rb, rg])
                        bk2, rb2 = nbank()
                        for f in range(4):
                            TR(bk2[:, f * 2:f * 2 + 2], gr[0:2, f * 128:(f + 1) * 128], identf[0:2, 0:2], [rg, r_identf], [rb2])
                        CP("act", modT[:, cg * 4:cg * 4 + 4, :], bk2[:, 0:8].rearrange("p (f s) -> p f s", s=2), [], [rb2, r_modT])
                TS("dve", modT[:, 8:16, :], modT[:, 8:16, :], 1.0, ALU.add, [r_modT], [r_modT])
                TS("dve", modT[:, 32:40, :], modT[:, 32:40, :], 1.0, ALU.add, [r_modT], [r_modT])
                S.flush()

            with contextlib.ExitStack() as stP:
                hT, r_hT = sb(stP, "hT_all", [128, 8, NTOK], BF16)
                wg = Rot(nc, stP, "wg", [128, 8, 512], BF16)
                with contextlib.ExitStack() as st:
                    junk, r_junk = sb(st, "f_junk", [128, D], BF16)
                    sss = Rot(nc, st, "f_ss", [128, 12], F32, n=3)
                    xns = Rot(nc, st, "f_xn", [128, 4, D], BF16, n=3)
                    xts = Rot(nc, st, "f_xt", [128, 4, D], F32, n=3)
                    hregs = [Region("hT%d" % i_) for i_ in range(len(TG))]

                    def front_g(gi_, tok0, ntok):
                        xt, r_xt = xts.next()
                        ss, r_ss = sss.next()
                        xn, r_xn = xns.next()
                        bufs = (junk, Region("junk"), ss, r_ss, xn, r_xn)
                        frontA(bufs, src_x, tok0, ntok, xt, r_xt)
                        yield
                        frontB(bufs, ntok, 0, 8, 1 if tok0 == 0 else 0, hT[:, :, tok0:tok0 + ntok], hregs[gi_])
                    pipe((front_g(gi_, t0_, n_) for gi_, (t0_, n_) in enumerate(TG)), 3)
                    S.flush()
                    r_hT = Region("hT_ro")

                groups = [("v", 512, 512), ("u", 0, 512), ("qb", 1024, 512), ("kv", 1536, 256), ("qkc", 1792, 512),
                          ("vc", 2304, 512), ("rc", 2816, 512), ("z", 3328, 32)]

                def load_wg(gi):
                    name, c0, ncol = groups[gi]
                    wt, rw = wg.next()
                    DMA("pool", wt[:, :, 0:ncol], w_in[l][:, c0:c0 + ncol].rearrange("(kc p) n -> p kc n", p=128), [], [rw])
                    return wt, rw

                def proj_tm(wt, rw, ncol, tcol0, kind="mm"):
                    bk, rb = nbank(kind)
                    for kc in range(8):
                        MM(bk[:, 0:ncol], hT[:, kc, tcol0:tcol0 + 128], wt[:, kc, 0:ncol], kc == 0, kc == 7, [r_hT, rw], [rb])
                    return bk, rb

                def proj_fm(wt, rw, c0, m, tok0, ntok, kind="mm"):
                    bk, rb = nbank(kind)
                    for kc in range(8):
                        MM(bk[0:m, 0:ntok], wt[:, kc, c0:c0 + m], hT[:, kc, tok0:tok0 + ntok], kc == 0, kc == 7, [r_hT, rw], [rb])
                    return bk, rb

                def rope_g(wk, src, r_src, H, Ct, St, r_tab, out_, r_out):
                    t1, r_t1 = wk["t1"].next()
                    t2, r_t2 = wk["t2"].next()
                    W = H * 64
                    TT("dve", t1[:, 0:W].rearrange("p (h d) -> p h d", d=64), src.rearrange("p (h d) -> p h d", d=64),
                       Ct.unsqueeze(1).to_broadcast([128, H, 64]), ALU.mult, [r_src, r_tab], [r_t1])
                    sv = src.rearrange("p (h a b c) -> p h a b c", a=2, b=2, c=16)
                    tv = t2[:, 0:W].rearrange("p (h a b c) -> p h a b c", a=2, b=2, c=16)
                    Sv = St.rearrange("p (a b c) -> p a b c", a=2, b=2, c=16)
                    for j in range(2):
                        TT("pool", tv[:, :, :, j, :], sv[:, :, :, 1 - j, :],
                           Sv[:, :, j, :].unsqueeze(1).to_broadcast([128, H, 2, 16]), ALU.mult, [r_src, r_tab], [r_t2])
                    yield
                    TT("dve", out_, t1[:, 0:W], t2[:, 0:W], ALU.add, [r_t1, r_t2], [r_out])

                nxt = load_wg(0)
                SIMPLE = ("qkc", "vc", "rc", "z")
                shared_st = contextlib.ExitStack()

                @contextlib.contextmanager
                def group_scope(gname_):
                    if gname_ in SIMPLE:
                        yield shared_st
                    else:
                        with contextlib.ExitStack() as st_:
                            yield st_

                for gi, (gname, gc0, gncol) in enumerate(groups):
                    wt, rw = nxt
                    if gi + 1 < len(groups):
                        nxt = load_wg(gi + 1)
                    with group_scope(gname) as st:
                        if gname == "v":
                            ND = 8
                            wk = norm_wk(st, "v_", n=ND)
                            gvs = Rot(nc, st, "v_gv", [128, 512], F32, n=ND)
                            vns = Rot(nc, st, "v_vn", [128, 4, 512], BF16, n=3)

                            def v_sub(tok0, ntok, sub, grp):
                                if sub == 0:
                                    grp["vn"], _, grp["regs"] = vns.next2()
                                    grp["left"] = ntok // 128
                                vn = grp["vn"]
                                r_vn = grp["regs"][sub]
                                bk, rb = proj_tm(wt, rw, 512, tok0 + sub * 128)
                                gv, r_gv = gvs.next()
                                ACTF(gv[:], bk[:], AF.Gelu, [], [rb, r_gv])
                                yield
                                res = None
                                for res in rstd_g(wk, gv[:], r_gv, 4, 128, False, pool_pow=True):
                                    yield
                                st4, r_st4 = res
                                TT("dve", vn[:, sub, :].rearrange("p (h d) -> p h d", d=128), gv[:].rearrange("p (h d) -> p h d", d=128),
                                   st4[:, 16:20].unsqueeze(2).to_broadcast([128, 4, 128]), ALU.mult, [r_gv, r_st4], [r_vn])
                                grp["left"] -= 1
                                if grp["left"] == 0:
                                    DMA("sp", vn_s[tok0:tok0 + ntok, :].rearrange("(s p) f -> p s f", p=128),
                                        vn[:, 0:ntok // 128, :], grp["regs"][0:ntok // 128], [])

                            def v_all():
                                for (tok0, ntok) in TG:
                                    grp = {}
                                    for sub in range(ntok // 128):
                                        yield v_sub(tok0, ntok, sub, grp)
                            pipe(v_all(), ND)
                        elif gname == "u":
                            wsg32, r_wsg32 = sb(st, "wsg32", [128, 4, 128], F32)
                            wsg, r_wsg = sb(st, "wsg", [128, 4, 128], BF16)
                            bsg, r_bsg = sb(st, "bsg", [128, 4, 128], F32)
                            sgcol, r_sgcol = sb(st, "sgcol", [128, 4], F32)
                            DMA("sp", wsg32[:], wsguT_d[l], [], [r_wsg32])
                            DMA("sp", bsg[:], bsgu_d[l], [], [r_bsg])
                            DMA("sp", sgcol[:], sgcol_d[l], [], [r_sgcol])
                            CP("dve", wsg[:], wsg32[:], [r_wsg32], [r_wsg])
                            vns = Rot(nc, st, "u_vn", [128, 4, 512], BF16, n=3)
                            gus = Rot(nc, st, "u_gu", [128, 512], BF16, n=4)
                            tms = Rot(nc, st, "u_tm", [128, 512], F32, n=4)
                            aTs = Rot(nc, st, "u_aT", [128, 4, 512], BF16, n=3)

                            u_pref = {}
                            u_next = {TG[i_][0]: TG[i_ + 1] for i_ in range(len(TG) - 1)}

                            def u_load(tok0, ntok):
                                vn_, r_vn_ = vns.next()
                                DMA("sp", vn_[:, 0:ntok // 128, :], vn_s[tok0:tok0 + ntok, :].rearrange("(s p) f -> p s f", p=128),
                                    [], [r_vn_])
                                u_pref[tok0] = (vn_, r_vn_)

                            def u_g(tok0, ntok, g, grp):
                                nsub = ntok // 128
                                if g == 0:
                                    if tok0 not in u_pref:
                                        u_load(tok0, ntok)
                                    grp["vn"], grp["r_vn"] = u_pref.pop(tok0)
                                    nx_ = u_next.get(tok0)
                                    if nx_ is not None:
                                        u_load(*nx_)
                                    grp["aT"], _, grp["regs"] = aTs.next2()
                                    grp["regs"] = grp["regs"][0:4]
                                    grp["left"] = 4
                                vn, r_vn, aT = grp["vn"], grp["r_vn"], grp["aT"]
                                bk, rb = proj_fm(wt, rw, g * 128, 128, tok0, ntok)
                                gu, r_gu = gus.next()
                                ACTF(gu[:, 0:ntok], bk[:, 0:ntok], AF.Gelu, [], [rb, r_gu])
                                bk2, rb2 = nbank("aux")
                                for sub in range(nsub):
                                    MM(bk2[:, sub * 128:(sub + 1) * 128], vn[:, sub, g * 128:(g + 1) * 128], wsg[:, g, :],
                                       True, True, [r_vn, r_wsg], [rb2])
                                yield
                                tm, r_tm = tms.next()
                                STT(tm[:, 0:ntok].rearrange("p (s q) -> p s q", q=128),
                                    bk2[:, 0:ntok].rearrange("p (s q) -> p s q", q=128), sgcol[:, g:g + 1],
                                    bsg[:, g, :].unsqueeze(1).to_broadcast([128, nsub, 128]), ALU.mult, ALU.add,
                                    [r_bsg, r_sgcol], [rb2, r_tm])
                                yield
                                TT("pool", aT[:, g, 0:ntok], tm[:, 0:ntok], gu[:, 0:ntok], ALU.mult, [r_tm, r_gu], [grp["regs"][g]])
                                grp["left"] -= 1
                                if grp["left"] == 0:
                                    DMA("sp", brT[GI[tok0], :, 0:4, 0:ntok],
                                        aT[:, :, 0:ntok], grp["regs"], [])

                            def u_all():
                                for (tok0, ntok) in TG:
                                    grp = {}
                                    for g in range(4):
                                        yield u_g(tok0, ntok, g, grp)
                            pipe(u_all(), 3)
                        elif gname in ("qb", "kv"):
                            isq = gname == "qb"
                            H = 8 if isq else 2
                            W = H * 64
                            ND = 7
                            wk = norm_wk(st, gname + "_", n=ND)
                            t1s = Rot(nc, st, gname + "_t1", [128, 512], F32, n=ND)
                            t2s = Rot(nc, st, gname + "_t2", [128, 512], F32, n=ND)
                            raws = Rot(nc, st, gname + "_raw", [128, 512], F32, n=ND)
                            qrs = Rot(nc, st, gname + "_qr", [128, 512], BF16, n=3)
                            qTgs = Rot(nc, st, gname + "_Tg", [64, H, 512], BF16, n=3)
                            gain, r_gain = sb(st, gname + "_gain", [128, 2, 64], F32)
                            DMA("sp", gain[:], (qg_d if isq else kg_d)[l], [], [r_gain])
                            tabA, r_tabA = sb(st, gname + "_tabA", [128, 2, 34, 64], F32)
                            DMA("sp", tabA[:, 0, :, :], ropeC_d, [], [r_tabA])
                            DMA("sp", tabA[:, 1, :, :], ropeS_d, [], [r_tabA])
                            for j_ in range(2):
                                TT("dve" if j_ else "pool", tabA[:, j_, :, :], tabA[:, j_, :, :],
                                   gain[:, j_, :].unsqueeze(1).to_broadcast([128, 34, 64]), ALU.mult, [r_gain], [r_tabA])
                            if not isq:
                                vxs = Rot(nc, st, "kv_vx", [128, 4, 2, 65], BF16, n=3)
                                for bi_ in range(len(vxs.bufs)):
                                    MSET("pool", vxs.bufs[bi_][:], 1.0, vxs.subregs[bi_])
                            dstT = (qT_s if isq else kT_s)

                            def qk_sub(tok0, ntok, sub, grp):
                                nsub = ntok // 128
                                ti0 = tok0 // 128
                                if sub == 0:
                                    grp["qTg"], _, grp["regs"] = qTgs.next2()
                                    grp["left"] = nsub
                                    if not isq:
                                        grp["vx"], r_vx_main, grp["vregs"] = vxs.next2()
                                qTg = grp["qTg"]
                                r_tab = r_tabA
                                tabC = tabA[:, 0, ti0 + sub, :]
                                tabS = tabA[:, 1, ti0 + sub, :]
                                bk, rb = proj_tm(wt, rw, gncol, tok0 + sub * 128)
                                raw, r_raw = raws.next()
                                CP("act", raw[:, 0:W], bk[:, 0:W], [], [rb, r_raw])
                                if not isq:
                                    CP("act", grp["vx"][:, sub, :, 0:64], bk[:, 128:256].rearrange("p (h d) -> p h d", d=64),
                                       [], [rb, grp["vregs"][sub]])
                                yield
                                src = raw[:, 0:W]
                                t1, r_t1 = t1s.next()
                                t2, r_t2 = t2s.next()
                                TT("dve", t1[:, 0:W].rearrange("p (h d) -> p h d", d=64), src.rearrange("p (h d) -> p h d", d=64),
                                   tabC.unsqueeze(1).to_broadcast([128, H, 64]), ALU.mult, [r_raw, r_tab], [r_t1])
                                sv = src.rearrange("p (h a b c) -> p h a b c", a=2, b=2, c=16)
                                tv = t2[:, 0:W].rearrange("p (h a b c) -> p h a b c", a=2, b=2, c=16)
                                Sv = tabS.rearrange("p (a b c) -> p a b c", a=2, b=2, c=16)
                                for j in range(2):
                                    TT("pool", tv[:, :, :, j, :], sv[:, :, :, 1 - j, :],
                                       Sv[:, :, j, :].unsqueeze(1).to_broadcast([128, H, 2, 16]), ALU.mult, [r_raw, r_tab], [r_t2])
                                res = None
                                for res in rstd_g(wk, src, r_raw, H, 64, False):
                                    yield
                                st4, r_st4 = res
                                TT("dve", t1[:, 0:W], t1[:, 0:W], t2[:, 0:W], ALU.add, [r_t2], [r_t1])
                                yield
                                qr, r_qr = qrs.next()
                                TT("dve", qr[:, 0:W].rearrange("p (h d) -> p h d", d=64), t1[:, 0:W].rearrange("p (h d) -> p h d", d=64),
                                   st4[:, 16:16 + H].unsqueeze(2).to_broadcast([128, H, 64]), ALU.mult, [r_t1, r_st4], [r_qr])
                                yield
                                bk2, rb2 = nbank("tr")
                                bv = bk2[:].bitcast(BF16)
                                for h_ in range(H):
                                    TR(bv[0:64, h_ * 128:(h_ + 1) * 128], qr[:, h_ * 64:(h_ + 1) * 64], identb[:],
                                       [r_qr, r_identb], [rb2])
                                yield
                                CP("act", qTg[:, :, sub * 128:(sub + 1) * 128],
                                   bv[0:64, 0:H * 128].rearrange("p (h t) -> p h t", t=128), [], [rb2, grp["regs"][sub]])
                                grp["left"] -= 1
                                if grp["left"] == 0:
                                    DMA("sp", dstT[:, tok0:tok0 + ntok].rearrange("(h d) t -> d h t", d=64), qTg[:, :, 0:ntok],
                                        grp["regs"][0:nsub], [])
                                    if not isq:
                                        DMA("sp", Vx_s[tok0:tok0 + ntok, :].rearrange("(s p) f -> p s f", p=128),
                                            grp["vx"][:, 0:nsub, :, :].rearrange("p s h d -> p s (h d)"), grp["vregs"][0:nsub], [])

                            def qk_all():
                                for (tok0, ntok) in TG:
                                    grp = {}
                                    for sub in range(ntok // 128):
                                        yield qk_sub(tok0, ntok, sub, grp)
                            pipe(qk_all(), ND)
                        elif gname == "qkc":
                            bufs = Rot(nc, st, "qkc_b", [128, 4, 512], F32)
                            for (tok0, ntok) in TG:
                                bf, r_bf = bufs.next()
                                for f in range(4):
                                    bk, rb = proj_fm(wt, rw, f * 128, 128, tok0, ntok)
                                    CP("act" if f % 2 else "dve", bf[:, f, 0:ntok], bk[:, 0:ntok], [], [rb, r_bf])
                                DMA("sp", qkc_s[:, tok0:tok0 + ntok].rearrange("(f p) t -> p f t", p=128), bf[:, :, 0:ntok],
                                    [r_bf], [])
                        elif gname in ("vc", "rc"):
                            bufs = Rot(nc, st, gname + "_b", [128, 4, 512], BF16)
                            dst = vc_s if gname == "vc" else rs_s
                            for (tok0, ntok) in TG:
                                nsub = ntok // 128
                                bf, r_bf = bufs.next()
                                for sub in range(nsub):
                                    bk, rb = proj_tm(wt, rw, 512, tok0 + sub * 128)
                                    if gname == "vc":
                                        CP("act" if sub % 2 else "dve", bf[:, sub, :], bk[:], [], [rb, r_bf])
                                    else:
                                        ACTF(bf[:, sub, :], bk[:], AF.Silu, [], [rb, r_bf])
                                DMA("sp", dst[tok0:tok0 + ntok, :].rearrange("(s p) f -> p s f", p=128), bf[:, 0:nsub, :],
                                    [r_bf], [])
                        elif gname == "z":
                            bufs = Rot(nc, st, "z_b", [16, 2, 512], F32)
                            for (tok0, ntok) in TG:
                                bf, r_bf = bufs.next()
                                for dn in range(2):
                                    bk, rb = proj_fm(wt, rw, dn * 16, 16, tok0, ntok)
                                    CP("dve", bf[:, dn, 0:ntok], bk[0:16, 0:ntok], [], [rb, r_bf])
                                DMA("sp", zT_s[:, tok0:tok0 + ntok].rearrange("(a z) t -> z a t", z=16), bf[:, :, 0:ntok],
                                    [r_bf], [])
                        else:
                            gidx = int(gname[1:])
                            bufs = Rot(nc, st, "g_b", [128, 4, 512], BF16)
                            for (tok0, ntok) in TG:
                                if last and tok0 == 0:
                                    continue
                                bf, r_bf = bufs.next()
                                for f in range(4):
                                    bk, rb = proj_fm(wt, rw, f * 128, 128, tok0, ntok)
                                    ACTF(bf[:, f, 0:ntok], bk[:, 0:ntok], AF.Sigmoid, [], [rb, r_bf])
                                DMA("sp", sig[GI[tok0], :, gidx * 4:(gidx + 1) * 4, 0:ntok],
                                    bf[:, :, 0:ntok], [r_bf], [])
                        if gname not in SIMPLE or gi == len(groups) - 1:
                            S.flush()
                shared_st.close()

            with contextlib.ExitStack() as st:
                wgt, r_wgt = sb(st, "wgt", [17, 2, 256], F32)
                rmask, r_rmask = sb(st, "rmask", [128, 512], F32)
                DMA("sp", wgt[0:16, :, :], wgate_d[l].rearrange("a z n -> z a n"), [], [r_wgt])
                DMA("sp", wgt[16:17, :, :], bgrow_d[l], [], [r_wgt])
                MSET("pool", rmask[:], 1.0, [r_rmask])
                MSET("pool", rmask[:].rearrange("p (c l) -> p c l", l=128)[:, :, 0:1], 0.0, [r_rmask])

                def gla_dir(dn):
                    P = "c%d_" % dn
                    zts = Rot(nc, st, P + "zt", [17, 256], F32, n=2)
                    for bi_ in range(2):
                        MSET("dve", zts.bufs[bi_][:], 1.0, [zts.regs[bi_]])
                    qks = Rot(nc, st, P + "qk", [128, 4, 256], F32, n=2)
                    vchs = Rot(nc, st, P + "vch", [128, 2, 512], BF16, n=3)
                    ex_, r_ex = sb(st, P + "ex", [128, 512], F32)
                    sp_, r_sp = sb(st, P + "sp", [128, 512], F32)
                    cs_, r_cs = sb(st, P + "cs", [128, 512], F32)
                    d1_, r_d1 = sb(st, P + "d1", [128, 512], F32)
                    d2_, r_d2 = sb(st, P + "d2", [128, 512], F32)
                    E = [sb(st, P + "E%d" % i_, [128, 512], F32) for i_ in range(4)]
                    decs = Rot(nc, st, P + "dec", [128, 2, 2], F32, n=2)
                    prods = Rot(nc, st, P + "prod", [128, 4, 2, 256], BF16, n=2)
                    sTs = Rot(nc, st, P + "sT", [128, 4, 128], BF16, n=2)
                    kots = Rot(nc, st, P + "kot", [128, 4, 64], BF16, n=2)
                    obufs = Rot(nc, st, P + "ob", [128, 2, 512], F32, n=2)
                    stt, r_stt = sb(st, P + "state", [128, 2, 128], F32)
                    stb, r_stb = sb(st, P + "stateb", [128, 2, 128], BF16)
                    MSET("dve", stt[:], 0.0, [r_stt])
                    MSET("dve", stb[:], 0.0, [r_stb])
                    mk, r_mk = (mle, r_mle) if dn == 0 else (mge, r_mge)
                    o_dst = of_s if dn == 0 else ob_s
                    order = TG256 if dn == 0 else [TG256[0]] + TG256[:0:-1]
                    iref = 64 if dn == 0 else 63
                    itot = 127 if dn == 0 else 0
                    loaded = {}

                    def load(gi_):
                        tok0, ntok = order[gi_]
                        zt, r_zt = zts.next()
                        qk, r_qk = qks.next()
                        vch, r_vch = vchs.next()
                        DMA("sp", zt[0:16, :], zT_s[dn * 16:(dn + 1) * 16, tok0:tok0 + ntok], [], [r_zt])
                        DMA("sp", qk[:], qkc_s[:, tok0:tok0 + ntok].rearrange("(f p) t -> p f t", p=128), [], [r_qk])
                        DMA("sp", vch[:], vc_s[tok0:tok0 + ntok, :].rearrange("(s p) f -> p s f", p=128), [], [r_vch])
                        loaded[gi_] = (zt, r_zt, qk, r_qk, vch, r_vch)

                    load(0)
                    state = {}

                    def prep(gi_):
                        zt, r_zt, qk, r_qk, vch, r_vch = loaded[gi_]
                        if gi_ + 1 < len(order):
                            load(gi_ + 1)
                        bk, rb = nbank("aux")
                        for pr in range(2):
                            MM(bk[:, pr * 256:(pr + 1) * 256], wgt[:, dn, pr * 128:(pr + 1) * 128], zt[:, :], True, True,
                               [r_wgt, r_zt], [rb])
                        ACTF(ex_[:], bk[:], AF.Exp, [], [rb, r_ex], scale=-1.0)
                        yield
                        ACTF(sp_[:], ex_[:], AF.Ln, [r_ex], [r_sp], bias=1.0)
                        yield
                        S.op("dve", lambda h: h.tensor_tensor_scan(out=cs_[:], data0=rmask[:], data1=sp_[:], initial=0.0,
                                                                   op0=ALU.mult, op1=ALU.add), [r_rmask, r_sp], [r_cs])
                        yield
                        csv = cs_[:].rearrange("p (c l) -> p c l", l=128)
                        cum, r_cum = cs_, r_cs
                        if dn == 1:
                            TT("pool", d1_[:], sp_[:], cs_[:], ALU.subtract, [r_sp, r_cs], [r_d1])
                            yield
                            TT("dve", sp_[:].rearrange("p (c l) -> p c l", l=128), d1_[:].rearrange("p (c l) -> p c l", l=128),
                               csv[:, :, 127:128].to_broadcast([128, 4, 128]), ALU.add, [r_d1, r_cs], [r_sp])
                            yield
                            csv = sp_[:].rearrange("p (c l) -> p c l", l=128)
                            cum, r_cum = sp_, r_sp
                        d1v = d1_[:].rearrange("p (c l) -> p c l", l=128)
                        d2v = d2_[:].rearrange("p (c l) -> p c l", l=128)
                        TT("dve", d1v, csv, csv[:, :, iref:iref + 1].to_broadcast([128, 4, 128]), ALU.subtract, [r_cum], [r_d1])
                        TT("pool", d2v, csv, csv[:, :, itot:itot + 1].to_broadcast([128, 4, 128]), ALU.subtract, [r_cum], [r_d2])
                        ACTF(E[2][0][:], cum[:], AF.Exp, [r_cum], [E[2][1]], scale=-1.0 / 16)
                        yield
                        prod, r_prod = prods.next()
                        dec, r_dec = decs.next()
                        ACTF(E[0][0][:], d1_[:], AF.Exp, [r_d1], [E[0][1]], scale=-1.0 / 16)
                        qv = qk[:, 0:2, :]
                        kv_ = qk[:, 2:4, :]
                        ev = lambda i_: E[i_][0][:].rearrange("p (r t) -> p r t", t=256)
                        STT(prod[:, 2, :, :], qv, 0.125, ev(2), ALU.mult, ALU.mult, [r_qk, E[2][1]], [r_prod])
                        yield
                        ACTF(E[1][0][:], d1_[:], AF.Exp, [r_d1], [E[1][1]], scale=1.0 / 16)
                        STT(prod[:, 0, :, :], qv, 0.125, ev(0), ALU.mult, ALU.mult, [r_qk, E[0][1]], [r_prod])
                        yield
                        ACTF(E[3][0][:], d2_[:], AF.Exp, [r_d2], [E[3][1]], scale=1.0 / 16)
                        TT("pool", prod[:, 1, :, :], kv_, ev(1), ALU.mult, [r_qk, E[1][1]], [r_prod])
                        yield
                        ACTF(dec[:], csv[:, :, itot].rearrange("p (r c) -> p r c", c=2), AF.Exp, [r_cum], [r_dec], scale=-1.0 / 16)
                        TT("pool", prod[:, 3, :, :], kv_, ev(3), ALU.mult, [r_qk, E[3][1]], [r_prod])
                        state[gi_] = (prod, r_prod, dec, r_dec, vch, r_vch)

                    def chunks(gi_):
                        tok0, ntok = order[gi_]
                        prod, r_prod, dec, r_dec, vch, r_vch = state.pop(gi_)
                        obuf, r_ob = obufs.next()
                        for ch in ((0, 1) if dn == 0 else (1, 0)):
                            c0 = ch * 128
                            bks = [nbank("mm"), nbank("mm")]
                            bkts = [nbank("mm"), nbank("mm")]
                            for h_ in range(4):
                                pr, hp = h_ // 2, h_ % 2
                                ps_ = slice(hp * 64, (hp + 1) * 64)
                                MM(bks[hp][0][:, pr * 128:(pr + 1) * 128], prod[ps_, 1, pr, c0:c0 + 128], prod[ps_, 0, pr, c0:c0 + 128],
                                   True, True, [r_prod], [bks[hp][1]])
                            for h_ in range(4):
                                pr, hp = h_ // 2, h_ % 2
                                ps_ = slice(hp * 64, (hp + 1) * 64)
                                TR(bkts[hp][0][:].bitcast(BF16)[:, pr * 64:(pr + 1) * 64], prod[ps_, 3, pr, c0:c0 + 128],
                                   identb[ps_, hp * 64:(hp + 1) * 64], [r_prod, r_identb], [bkts[hp][1]])
                            yield
                            sT, r_sT = sTs.next()
                            kot, r_kot = kots.next()
                            sTv = sT[:].rearrange("p (r q) l -> p r q l", q=2)
                            kotv = kot[:].rearrange("p (r q) d -> p r q d", q=2)
                            for hp in range(2):
                                TT("dve", sTv[:, :, hp, :], bks[hp][0][:, 0:256].rearrange("p (r l) -> p r l", l=128),
                                   mk[:].unsqueeze(1).to_broadcast([128, 2, 128]), ALU.mult, [r_mk], [bks[hp][1], r_sT])
                                CP("act", kotv[:, :, hp, :], bkts[hp][0][:].bitcast(BF16)[:, 0:128].rearrange("p (r d) -> p r d", d=64),
                                   [], [bkts[hp][1], r_kot])
                            yield
                            bko, rbo = nbank("aux")
                            for h_ in range(4):
                                pr, hp = h_ // 2, h_ % 2
                                ps_ = slice(hp * 64, (hp + 1) * 64)
                                MM(bko[:, h_ * 128:(h_ + 1) * 128], sT[:, h_, :], vch[:, ch, h_ * 128:(h_ + 1) * 128], True, False,
                                   [r_sT, r_vch], [rbo])
                                MM(bko[:, h_ * 128:(h_ + 1) * 128], prod[ps_, 2, pr, c0:c0 + 128], stb[ps_, pr, :], False, True,
                                   [r_prod, r_stb], [rbo])
                            bku, rbu = nbank("tr")
                            for h_ in range(4):
                                pr, hp = h_ // 2, h_ % 2
                                MM(bku[hp * 64:(hp + 1) * 64, pr * 128:(pr + 1) * 128], kot[:, h_, :], vch[:, ch, h_ * 128:(h_ + 1) * 128],
                                   True, True, [r_kot, r_vch], [rbu])
                            TT("pool", stt[:], stt[:], dec[:, :, ch].unsqueeze(2).to_broadcast([128, 2, 128]), ALU.mult,
                               [r_dec], [r_stt])
                            yield
                            TT("dve", stt[:], stt[:], bku[:, 0:256].rearrange("p (r v) -> p r v", v=128), ALU.add, [], [rbu, r_stt])
                            CP("act" if dn == 0 else "dve", obuf[:, ch, :], bko[:], [], [rbo, r_ob])
                            yield
                            CP("act", stb[:], stt[:], [r_stt], [r_stb])
                            yield
                        DMA("sp", o_dst[tok0:tok0 + ntok, :].rearrange("(s p) f -> p s f", p=128), obuf[:], [r_ob], [])

                    def seq():
                        for _ in prep(0):
                            yield
                        for gi_ in range(len(order)):
                            c_ = chunks(gi_)
                            p_ = prep(gi_ + 1) if gi_ + 1 < len(order) else None
                            while c_ is not None or p_ is not None:
                                if c_ is not None:
                                    try:
                                        next(c_)
                                    except StopIteration:
                                        c_ = None
                                if p_ is not None:
                                    try:
                                        next(p_)
                                    except StopIteration:
                                        p_ = None
                                yield
                    return seq()

                for _ in itertools.zip_longest(gla_dir(0), gla_dir(1)):
                    pass
                S.flush()

            stW = contextlib.ExitStack()
            wbr, r_wbr = sb(stW, "wbr", [128, 12, D], BF16)
            wo, r_wo = sb(stW, "wo", [128, 8, D], BF16)
            wgs, r_wgs = sb(stW, "wgs", [128, 8, 3072], BF16)

            mw_pieces = []
            for k6 in range(6):
                mw_pieces.append((wgs[:, :, k6 * 512:(k6 + 1) * 512],
                                  w_in[l][:, 3360 + k6 * 512:3360 + (k6 + 1) * 512].rearrange("(kc p) n -> p kc n", p=128), r_wgs))
            for k in range(3):
                mw_pieces.append((wbr[:, k * 4:(k + 1) * 4, :], w_br[l][k].rearrange("(wc p) n -> p wc n", p=128), r_wbr))
            for k2 in range(2):
                mw_pieces.append((wo[:, :, k2 * 512:(k2 + 1) * 512],
                                  w_o[l][:, k2 * 512:(k2 + 1) * 512].rearrange("(jc p) n -> p jc n", p=128), r_wo))

            def issue_merge_piece():
                if mw_pieces:
                    o_, i_, r_ = mw_pieces.pop(0)
                    DMA("pool", o_, i_, [], [r_])

            with contextlib.ExitStack() as st:
                kTa, r_kTa = sb(st, "kTa", [64, 2, NTOK], BF16)
                Vxa, r_Vxa = sb(st, "Vxa", [128, 34, 130], BF16)
                sk32, r_sk32 = sb(st, "sk32", [128, 8], F32)
                esk, r_esk = sb(st, "esk", [128, 8], F32)
                DMA("sp", kTa[:], kT_s.rearrange("(h d) t -> d h t", d=64), [], [r_kTa])
                DMA("sp", Vxa[:], Vx_s.rearrange("(s p) f -> p s f", p=128), [], [r_Vxa])
                DMA("sp", sk32[:], sink_d[l], [], [r_sk32])
                ACTF(esk[:], sk32[:], AF.Exp, [r_sk32], [r_esk])
                qTt = Rot(nc, st, "b_qT", [64, 8, 512], BF16, n=3)
                pts = Rot(nc, st, "b_pt", [128, 512], BF16, n=24)
                dens = Rot(nc, st, "b_den", [128, 8], F32, n=8)
                bos = Rot(nc, st, "b_bo", [128, 512], BF16, n=4)
                boTs = Rot(nc, st, "b_boT", [128, 4, 512], BF16, n=3)

                sgroups = [(t0_, n_) for (t0_, n_) in TG if not (last and t0_ == 0)]
                swa_next = {sgroups[i_][0]: sgroups[i_ + 1] for i_ in range(len(sgroups) - 1)}
                qT_pref = {}

                def swa_qload(tok0, ntok):
                    qT_, r_qT_ = qTt.next()
                    DMA("sp", qT_[:, :, 0:ntok], qT_s[:, tok0:tok0 + ntok].rearrange("(h d) t -> d h t", d=64), [], [r_qT_])
                    qT_pref[tok0] = (qT_, r_qT_)

                def swa_tile(tok0, ntok, sub, grp):
                    nsub = ntok // 128
                    if sub == 0:
                        if tok0 not in qT_pref:
                            swa_qload(tok0, ntok)
                        grp["qT"], grp["r_qT"] = qT_pref.pop(tok0)
                        nx_ = swa_next.get(tok0)
                        if nx_ is not None:
                            swa_qload(*nx_)
                        grp["boT"], _, grp["regs"] = boTs.next2()
                        grp["left"] = nsub
                    qT, r_qT, boT = grp["qT"], grp["r_qT"], grp["boT"]
                    i = tok0 // 128 + sub
                    if i < 2:
                        kbs = [(0, None), (1, None)]
                    else:
                        kbs = []
                        if i - 1 >= 2:
                            kbs.append((i - 1, "ge"))
                        kbs.append((i, None))
                        if i + 1 <= 33:
                            kbs.append((i + 1, "le"))
                        kbs += [(0, None), (1, None)]
                    bo, r_bo = bos.next()

                    def A(kh):
                        plist = []
                        for (kb, m) in kbs:
                            bk, rb = nbank("mm")
                            MM(bk[:], kTa[:, kh, kb * 128:(kb + 1) * 128], qT[:, 4 * kh:4 * kh + 4, sub * 128:(sub + 1) * 128],
                               True, True, [r_kTa, r_qT], [rb])
                            pt, r_pt = pts.next()
                            ACTF(pt[:], bk[:], AF.Exp, [], [rb, r_pt], scale=0.125)
                            if m is not None:
                                mk, r_mk = (mge, r_mge) if m == "ge" else (mle, r_mle)
                                TT("pool", pt[:].rearrange("p (h q) -> p h q", q=128), pt[:].rearrange("p (h q) -> p h q", q=128),
                                   mk[:].unsqueeze(1).to_broadcast([128, 4, 128]), ALU.mult, [r_mk], [r_pt])
                            plist.append((pt, r_pt, kb))
                        return plist

                    def B(kh, plist):
                        bko, rbo = nbank("aux")
                        for hh in range(4):
                            for idx, (pt, r_pt, kb) in enumerate(plist):
                                MM(bko[:, hh * 128:hh * 128 + 65], pt[:, hh * 128:(hh + 1) * 128], Vxa[:, kb, kh * 65:(kh + 1) * 65],
                                   idx == 0, idx == len(plist) - 1, [r_pt, r_Vxa], [rbo])
                        return bko, rbo

                    def C(kh, bko, rbo):
                        den, r_den = dens.next()
                        pov = bko[:].rearrange("p (h d) -> p h d", d=128)
                        TT("dve", den[:, 0:4], pov[:, :, 64], esk[:, 4 * kh:4 * kh + 4], ALU.add, [r_esk], [rbo, r_den])
                        RCP(den[:, 4:8], den[:, 0:4], [r_den], [r_den])
                        TT("dve", bo[:, kh * 256:(kh + 1) * 256].rearrange("p (h d) -> p h d", d=64), pov[:, :, 0:64],
                           den[:, 4:8].unsqueeze(2).to_broadcast([128, 4, 64]), ALU.mult, [r_den], [rbo, r_bo])

                    pl0 = A(0)
                    yield
                    pl1 = A(1)
                    po0 = B(0, pl0)
                    yield
                    C(0, *po0)
                    po1 = B(1, pl1)
                    yield
                    C(1, *po1)
                    yield
                    bk2, rb2 = nbank("tr")
                    bv = bk2[:].bitcast(BF16)
                    for c in range(4):
                        TR(bv[:, c * 128:(c + 1) * 128], bo[:, c * 128:(c + 1) * 128], identb[:], [r_bo, r_identb], [rb2])
                    yield
                    CP("act", boT[:, :, sub * 128:(sub + 1) * 128], bv[:, 0:512].rearrange("p (c t) -> p c t", t=128),
                       [], [rb2, grp["regs"][sub]])
                    grp["left"] -= 1
                    if grp["left"] == 0:
                        DMA("sp", brT[GI[tok0], :, 4:8, 0:ntok], boT[:, :, 0:ntok],
                            grp["regs"][0:nsub], [])

                def swa_all():
                    nt_ = 0
                    for (tok0, ntok) in sgroups:
                        grp = {}
                        for sub in range(ntok // 128):
                            yield swa_tile(tok0, ntok, sub, grp)
                            nt_ += 1
                            if nt_ >= 3 and nt_ % 2 == 1:
                                issue_merge_piece()
                    while mw_pieces:
                        issue_merge_piece()
                pipe(swa_all(), 3)
                S.flush()

            with contextlib.ExitStack() as st:
                ND = 5
                wk = norm_wk(st, "c3_", n=ND)
                ofs = Rot(nc, st, "c3_of", [128, 4, 512], F32, n=3)
                obs = Rot(nc, st, "c3_ob", [128, 4, 512], F32, n=3)
                rss = Rot(nc, st, "c3_rs", [128, 4, 512], BF16, n=3)
                os_ = Rot(nc, st, "c3_o", [128, 512], F32, n=ND)
                cbs = Rot(nc, st, "c3_cb", [128, 512], BF16, n=3)
                cTs = Rot(nc, st, "c3_cT", [128, 4, 512], BF16, n=2)

                c3groups = [(t0_, n_) for (t0_, n_) in TG if not (last and t0_ == 0)]
                c3_next = {c3groups[i_][0]: c3groups[i_ + 1] for i_ in range(len(c3groups) - 1)}
                c3_pref = {}

                def c3_load(tok0, ntok):
                    nsub = ntok // 128
                    of_, r_of_ = ofs.next()
                    ob_, r_ob_ = obs.next()
                    rs_, r_rs_ = rss.next()
                    DMA("sp", of_[:, 0:nsub, :], of_s[tok0:tok0 + ntok, :].rearrange("(s p) f -> p s f", p=128), [], [r_of_])
                    DMA("sp", ob_[:, 0:nsub, :], ob_s[tok0:tok0 + ntok, :].rearrange("(s p) f -> p s f", p=128), [], [r_ob_])
                    DMA("sp", rs_[:, 0:nsub, :], rs_s[tok0:tok0 + ntok, :].rearrange("(s p) f -> p s f", p=128), [], [r_rs_])
                    c3_pref[tok0] = (of_, r_of_, ob_, r_ob_, rs_, r_rs_)

                def c3_sub(tok0, ntok, sub, grp):
                    nsub = ntok // 128
                    if sub == 0:
                        if tok0 not in c3_pref:
                            c3_load(tok0, ntok)
                        (grp["of"], grp["r_of"], grp["ob"], grp["r_ob"], grp["rs"], grp["r_rs"]) = c3_pref.pop(tok0)
                        nx_ = c3_next.get(tok0)
                        if nx_ is not None:
                            c3_load(*nx_)
                        grp["cT"], _, grp["regs"] = cTs.next2()
                        grp["left"] = nsub
                    o_, r_o = os_.next()
                    TT("dve", o_[:], grp["of"][:, sub, :], grp["ob"][:, sub, :], ALU.add, [grp["r_of"], grp["r_ob"]], [r_o])
                    yield
                    res = None
                    for res in rstd_g(wk, o_[:], r_o, 4, 128, True):
                        yield
                    st4, r_st4 = res
                    TT("dve", o_[:].rearrange("p (h d) -> p h d", d=128), o_[:].rearrange("p (h d) -> p h d", d=128),
                       st4[:, 16:20].unsqueeze(2).to_broadcast([128, 4, 128]), ALU.mult, [r_st4], [r_o])
                    yield
                    cb, r_cb = cbs.next()
                    TT("pool", cb[:], o_[:], grp["rs"][:, sub, :], ALU.mult, [r_o, grp["r_rs"]], [r_cb])
                    yield
                    bk2, rb2 = nbank("tr")
                    bv = bk2[:].bitcast(BF16)
                    for c in range(4):
                        TR(bv[:, c * 128:(c + 1) * 128], cb[:, c * 128:(c + 1) * 128], identb[:], [r_cb, r_identb], [rb2])
                    yield
                    CP("act", grp["cT"][:, :, sub * 128:(sub + 1) * 128], bv[:, 0:512].rearrange("p (c t) -> p c t", t=128),
                       [], [rb2, grp["regs"][sub]])
                    grp["left"] -= 1
                    if grp["left"] == 0:
                        DMA("sp", brT[GI[tok0], :, 8:12, 0:ntok], grp["cT"][:, :, 0:ntok],
                            grp["regs"][0:nsub], [])

                def c3_all():
                    for (tok0, ntok) in TG:
                        if last and tok0 == 0:
                            continue
                        grp = {}
                        for sub in range(ntok // 128):
                            yield c3_sub(tok0, ntok, sub, grp)
                pipe(c3_all(), ND)
                S.flush()

            with contextlib.ExitStack() as st:
                brts = Rot(nc, st, "m_br", [128, 12, 512], BF16)
                xts = Rot(nc, st, "m_xt", [128, 4, D], F32)
                junk, r_junk = sb(st, "m_junk", [128, D], BF16)
                sss = Rot(nc, st, "m_ss", [128, 12], F32, n=2)
                xn, r_xn = sb(st, "m_xn", [128, 4, D], BF16)
                hTm, r_hTm = sb(st, "m_hT", [128, 8, 512], BF16)
                sgb = Rot(nc, st, "m_sgb", [128, 512], BF16, n=4)
                tms = Rot(nc, st, "m_tm", [128, 512], F32, n=4)
                yT, r_yT = sb(st, "m_yT", [128, 8, 512], BF16)
                mgroups = [(t0_, n_) for (t0_, n_) in TG if not (last and t0_ == 0)]

                def m_load(tok0, ntok):
                    brt, r_brt = brts.next()
                    xt, r_xt = xts.next()
                    ss, r_ss = sss.next()
                    DMA("sp", brt[:, :, 0:ntok], brT[GI[tok0], :, :, 0:ntok], [], [r_brt])
                    return (brt, r_brt, xt, r_xt, ss, r_ss)

                wo_scaled = [False]
                glcol, r_glcol = sb(st, "glcol", [128, 1], F32)
                DMA("sp", glcol[:], glcol_d[l], [], [r_glcol])
                for wc in range(4):
                    TS("dve" if wc % 2 else "pool", wbr[:, 8 + wc, :], wbr[:, 8 + wc, :], glcol[:, 0:1], ALU.mult, [r_glcol], [r_wbr])
                ld = m_load(*mgroups[0])
                bufsA = (junk, Region("junk"), ld[4], ld[5], xn, r_xn)
                frontA(bufsA, src_x, mgroups[0][0], mgroups[0][1], ld[2], ld[3])
                for mi, (tok0, ntok) in enumerate(mgroups):
                    nsub = ntok // 128
                    s = 1 if tok0 == 0 else 0
                    brt, r_brt, xt, r_xt, ss, r_ss = ld
                    frontB((junk, None, ss, r_ss, xn, r_xn), ntok, 0, 8, s, hTm[:, :, 0:ntok], r_hTm)
                    if mi + 1 < len(mgroups):
                        ld = m_load(*mgroups[mi + 1])
                    if s == 0 and not wo_scaled[0]:
                        for jc in range(8):
                            TT("pool" if jc % 2 else "dve", wo[:, jc, :], wo[:, jc, :], gbc[:, 0, 0, :], ALU.mult, [r_gbc], [r_wo])
                        wo_scaled[0] = True
                    for j in range(8):
                        tks = []
                        for k in range(3):
                            bkg, rbg = nbank("mm")
                            c0 = k * 1024 + j * 128
                            for kc in range(8):
                                MM(bkg[:, 0:ntok], wgs[:, kc, c0:c0 + 128], hTm[:, kc, 0:ntok], kc == 0, kc == 7, [r_wgs, r_hTm], [rbg])
                            sg, r_sg = sgb.next()
                            ACTF(sg[:, 0:ntok], bkg[:, 0:ntok], AF.Sigmoid, [], [rbg, r_sg])
                            bk, rb = nbank("aux")
                            for wc in range(4):
                                MM(bk[:, 0:ntok], wbr[:, k * 4 + wc, j * 128:(j + 1) * 128], brt[:, k * 4 + wc, 0:ntok], wc == 0, wc == 3,
                                   [r_wbr, r_brt], [rb])
                            tm, r_tm = tms.next()
                            TT("dve", tm[:, 0:ntok], bk[:, 0:ntok], sg[:, 0:ntok], ALU.mult, [r_sg], [rb, r_tm])
                            tks.append((tm, r_tm))
                        TT("pool", tks[0][0][:, 0:ntok], tks[0][0][:, 0:ntok], tks[1][0][:, 0:ntok], ALU.add, [tks[1][1]], [tks[0][1]])
                        TT("pool", yT[:, j, 0:ntok], tks[0][0][:, 0:ntok], tks[2][0][:, 0:ntok], ALU.add, [tks[0][1], tks[2][1]], [r_yT])
                    if mi + 1 < len(mgroups):
                        frontA((junk, None, ld[4], ld[5], xn, r_xn), src_x, mgroups[mi + 1][0], mgroups[mi + 1][1], ld[2], ld[3])
                    for sub in range(nsub):
                        for half in range(2):
                            bk, rb = nbank("tr")
                            for jc in range(8):
                                MM(bk[:], yT[:, jc, sub * 128:(sub + 1) * 128], wo[:, jc, half * 512:(half + 1) * 512], jc == 0, jc == 7,
                                   [r_yT, r_wo], [rb])
                            xs_ = xt[:, sub, half * 512:(half + 1) * 512]
                            if s == 1:
                                tm, r_tm = tms.next()
                                TT("dve", tm[:], bk[:], gbc[:, 0, s, half * 512:(half + 1) * 512], ALU.mult, [r_gbc], [rb, r_tm])
                                TT("pool", xs_, tm[:], xs_, ALU.add, [r_tm], [r_xt])
                            else:
                                TT("dve", xs_, bk[:], xs_, ALU.add, [], [rb, r_xt])
                    DMA("sp", xm[tok0:tok0 + ntok, :].rearrange("(s p) d -> p s d", p=128), xt[:, 0:nsub, :], [r_xt], [])
                S.flush()

            stW.close()

            with contextlib.ExitStack() as st:
                w1, r_w1 = sb(st, "w1", [128, 8, 4096], BF16)
                w2, r_w2 = sb(st, "w2", [128, 32, D], BF16)
                r_w1h = [Region("w1h%d" % i_) for i_ in range(2)]
                r_w2h = [Region("w2h%d" % i_) for i_ in range(2)]
                for hf in range(2):
                    DMA("pool", w1[:, :, hf * 2048:(hf + 1) * 2048],
                        w_ff1[l][:, hf * 2048:(hf + 1) * 2048].rearrange("(kc p) n -> p kc n", p=128), [], [r_w1h[hf]])
                for hf in range(2):
                    DMA("pool", w2[:, hf * 16:(hf + 1) * 16, :],
                        w_ff2[l][hf * 2048:(hf + 1) * 2048, :].rearrange("(fc p) n -> p fc n", p=128), [], [r_w2h[hf]])
                junk, r_junk = sb(st, "F_junk", [128, D], BF16)
                sss = Rot(nc, st, "F_ss", [128, 12], F32, n=2)
                xns = Rot(nc, st, "F_xn", [128, 2, D], BF16, n=2)
                xts = Rot(nc, st, "F_xt", [128, 2, D], F32, n=2)
                h2s = Rot(nc, st, "F_h2", [128, 8, 256], BF16, n=2)
                f1, r_f1 = sb(st, "F_f1", [128, 32, 256], BF16)
                rts = Rot(nc, st, "F_rt", [128, 256], F32, n=3)
                tms = Rot(nc, st, "F_tm", [128, 512], F32, n=2)
                fgroups = [(t0_, n_) for (t0_, n_) in TG256 if not (last and t0_ == 0)]

                def F_A(tok0, ntok):
                    ctx_ = {"tok0": tok0, "ntok": ntok, "s": 1 if tok0 == 0 else 0}
                    ctx_["xt"], ctx_["r_xt"] = xts.next()
                    ss, r_ss = sss.next()
                    xn, r_xn = xns.next()
                    ctx_["bufs"] = (junk, Region("junk"), ss, r_ss, xn, r_xn)
                    frontA(ctx_["bufs"], xm, tok0, ntok, ctx_["xt"], ctx_["r_xt"])
                    return ctx_

                def F_B(c_):
                    c_["h2"], c_["r_h2"] = h2s.next()
                    frontB(c_["bufs"], c_["ntok"], 24, 32, c_["s"], c_["h2"][:, :, 0:c_["ntok"]], c_["r_h2"])

                def F_ff1(c_):
                    ntok, h2, r_h2 = c_["ntok"], c_["h2"], c_["r_h2"]
                    for fc in range(32):
                        bk, rb = nbank("mm")
                        for kc in range(8):
                            MM(bk[:, 0:ntok], w1[:, kc, fc * 128:(fc + 1) * 128], h2[:, kc, 0:ntok], kc == 0, kc == 7, [r_w1h[fc // 16], r_h2], [rb])
                        rt, r_rt = rts.next()
                        ACTF(rt[:, 0:ntok], bk[:, 0:ntok], AF.Relu, [], [rb, r_rt])
                        TT("dve" if fc % 2 else "pool", f1[:, fc, 0:ntok], rt[:, 0:ntok], rt[:, 0:ntok], ALU.mult, [r_rt], [r_f1])

                def F_ff2(c_):
                    tok0, ntok, s, xt, r_xt = c_["tok0"], c_["ntok"], c_["s"], c_["xt"], c_["r_xt"]
                    for sub in range(2):
                        for half in range(2):
                            bk, rb = nbank("aux")
                            for fc in range(32):
                                MM(bk[:], f1[:, fc, sub * 128:(sub + 1) * 128], w2[:, fc, half * 512:(half + 1) * 512], fc == 0, fc == 31,
                                   [r_f1, r_w2h[fc // 16]], [rb])
                            tm, r_tm = tms.next()
                            TT("dve", tm[:], bk[:], gbc[:, 1, s, half * 512:(half + 1) * 512], ALU.mult, [r_gbc], [rb, r_tm])
                            TT("pool", xt[:, sub, half * 512:(half + 1) * 512], tm[:], xt[:, sub, half * 512:(half + 1) * 512], ALU.add,
                               [r_tm], [r_xt])
                    if last:
                        DMA("sp", out[tok0 - NCTX:tok0 - NCTX + ntok, :].rearrange("(s p) d -> p s d", p=128), xt[:], [r_xt], [])
                    else:
                        DMA("sp", xa[tok0:tok0 + ntok, :].rearrange("(s p) d -> p s d", p=128), xt[:], [r_xt], [])

                cur = F_A(*fgroups[0])
                F_B(cur)
                for gi_ in range(len(fgroups)):
                    F_ff1(cur)
                    nxt_c = F_A(*fgroups[gi_ + 1]) if gi_ + 1 < len(fgroups) else None
                    F_ff2(cur)
                    if nxt_c is not None:
                        F_B(nxt_c)
                    cur = nxt_c
                S.flush()
    return nc


def _rope_tables():
    n_freq = 16
    freqs = (10000.0 ** (-np.arange(n_freq, dtype=np.float32) / n_freq)).astype(np.float32)
    t = np.arange(NLAT)
    row = (t // 64).astype(np.float32)
    col = (t % 64).astype(np.float32)
    ar = row[:, None] * freqs[None, :]
    ac = col[:, None] * freqs[None, :]
    C = np.concatenate([np.cos(ar), np.cos(ar), np.cos(ac), np.cos(ac)], axis=1).astype(np.float32)
    Sg = np.concatenate([-np.sin(ar), np.sin(ar), -np.sin(ac), np.sin(ac)], axis=1).astype(np.float32)
    Call = np.concatenate([np.ones((NCTX, 64), np.float32), C], axis=0)
    Sall = np.concatenate([np.zeros((NCTX, 64), np.float32), Sg], axis=0)
    Ct = np.ascontiguousarray(Call.reshape(34, 128, 64).transpose(1, 0, 2))
    St = np.ascontiguousarray(Sall.reshape(34, 128, 64).transpose(1, 0, 2))
    return Ct, St


def _prep(inputs):
    f = lambda a: np.ascontiguousarray(np.asarray(a, dtype=np.float32))
    x, c, ctx, c_ctx = f(inputs["x"]), f(inputs["c"]), f(inputs["ctx"]), f(inputs["c_ctx"])
    B = x.shape[0]
    bc = lambda a, p=128: np.ascontiguousarray(np.broadcast_to(a, (DEPTH, p) + a.shape[2:]))
    b_ada = f(inputs["b_ada"])
    q_norm, k_norm = f(inputs["q_norm"]), f(inputs["k_norm"])
    shared = {
        "w_ada": f(inputs["w_ada"]),
        "badaT": np.ascontiguousarray(b_ada.reshape(DEPTH, 48, 128).transpose(0, 2, 1)),
        "badaR": np.ascontiguousarray(np.broadcast_to(b_ada[:, None, :], (DEPTH, 2, 6 * D))),
        "w_in": f(inputs["w_in"]),
        "qg64": bc(np.stack([q_norm, q_norm.reshape(DEPTH, 2, 2, 16)[:, :, ::-1, :].reshape(DEPTH, 64)], axis=1)[:, None, :, :]),
        "kg64": bc(np.stack([k_norm, k_norm.reshape(DEPTH, 2, 2, 16)[:, :, ::-1, :].reshape(DEPTH, 64)], axis=1)[:, None, :, :]),
        "sinkb": bc(f(inputs["sink"])[:, None, :]),
        "sgcol": np.ascontiguousarray(f(inputs["sgu_norm"]).reshape(DEPTH, 4, 128).transpose(0, 2, 1)),
        "wsguT": np.ascontiguousarray(f(inputs["w_sgu"]).transpose(0, 3, 1, 2)),
        "bsgub": bc(f(inputs["b_sgu"])[:, None, :, :]),
        "wgate": np.ascontiguousarray(np.stack([f(inputs["w_gate_f"]), f(inputs["w_gate_b"])], axis=1)),
        "bgrow": np.ascontiguousarray(np.stack([f(inputs["b_gate_f"]), f(inputs["b_gate_b"])], axis=1)[:, None, :, :]),
        "glcol": np.ascontiguousarray(f(inputs["gla_norm"])[:, :, None]),
        "w_br": f(inputs["w_br"]), "w_o": f(inputs["w_o"]), "w_ff1": f(inputs["w_ff1"]), "w_ff2": f(inputs["w_ff2"]),
        "ident": np.eye(128, dtype=np.float32),
        "mask_ge": (np.arange(128)[:, None] >= np.arange(128)[None, :]).astype(np.float32),
        "mask_le": (np.arange(128)[:, None] <= np.arange(128)[None, :]).astype(np.float32),
        "sel": np.ascontiguousarray(np.broadcast_to(np.eye(2, dtype=np.float32)[:, :, None], (2, 2, 128)).transpose(1, 0, 2)),
    }
    shared["ropeC"], shared["ropeS"] = _rope_tables()
    shared["sel"] = np.ascontiguousarray(np.broadcast_to(np.eye(2, dtype=np.float32)[:, :, None], (2, 2, 128)))
    in_maps = []
    for b in range(B):
        m = dict(shared)
        m["xcat"] = np.ascontiguousarray(np.concatenate([ctx[b], x[b]], axis=0))
        cm = np.stack([c[b].reshape(8, 128).T, c_ctx.reshape(8, 128).T], axis=-1)
        m["cmod"] = np.ascontiguousarray(cm)
        in_maps.append(m)
    return in_maps


_NC_CACHE = {}


def kernel(**inputs):
    in_maps = _prep(inputs)
    if "nc" not in _NC_CACHE:
        _NC_CACHE["nc"] = build()
    nc = _NC_CACHE["nc"]
    res = run_bass_kernel_spmd(nc, in_maps, core_ids=list(range(len(in_maps))))
    return np.stack([np.asarray(r["out"], dtype=np.float32) for r in res.results], axis=0)
```

```python
import contextlib
import itertools
import numpy as np
import concourse.bass as bass
import concourse.mybir as mybir
from concourse.bass_utils import run_bass_kernel_spmd

F32 = mybir.dt.float32
BF16 = mybir.dt.bfloat16
F32R = mybir.dt.float32r
AF = mybir.ActivationFunctionType
ALU = mybir.AluOpType
AX = mybir.AxisListType

NTOK = 4352
NLAT = 4096
NCTX = 256
D = 1024
DEPTH = 2
EPS = 1e-6
TG = [(0, 256)] + [(256 + 512 * i, 512) for i in range(8)]
TG256 = [(256 * i, 256) for i in range(17)]
GI = {t0: i for i, (t0, _) in enumerate(TG)}


class Region:
    __slots__ = ("name", "last_write", "reads")

    def __init__(self, name=""):
        self.name = name
        self.last_write = None
        self.reads = []


class _Eng:
    def __init__(self, key):
        self.key = key
        self.prog = []
        self.count = 0
        self.seen = {}


class Sched:
    NRING = 28

    def __init__(self, nc, stack):
        self.nc = nc
        self.engs = {k: _Eng(k) for k in ("pe", "dve", "act", "pool", "sp")}
        self.ring_n = [0] * self.NRING
        self.ring_next = 0
        self.sems = {}
        for k in self.engs:
            self.sems[k] = stack.enter_context(nc.semaphore("s_" + k))
        for j in range(self.NRING):
            self.sems["ring%d" % j] = stack.enter_context(nc.semaphore("s_ring%d" % j))

    def _deps(self, reads, writes):
        deps = []
        for r in reads:
            if r.last_write is not None:
                deps.append(r.last_write)
        for w in writes:
            if w.last_write is not None:
                deps.append(w.last_write)
            deps.extend(w.reads)
        return deps

    def _emit_waits(self, eng, deps):
        best = {}
        for (k, v) in deps:
            if k == "pe" and eng.key == "pe":
                continue
            if best.get(k, 0) < v:
                best[k] = v
        for k, v in best.items():
            if eng.seen.get(k, 0) >= v:
                continue
            eng.seen[k] = v
            eng.prog.append(("wait", k, v))

    def _record(self, tok, reads, writes):
        for r in reads:
            r.reads.append(tok)
        for w in writes:
            w.last_write = tok
            w.reads = []

    def op(self, ekey, fn, reads=(), writes=()):
        eng = self.engs[ekey]
        self._emit_waits(eng, self._deps(reads, writes))
        eng.count += 1
        tok = (ekey, eng.count)
        eng.prog.append(("op", fn, ekey))
        self._record(tok, reads, writes)
        return tok

    def dma(self, qkey, fn, reads=(), writes=()):
        eng = self.engs[qkey]
        j = self.ring_next
        self.ring_next = (self.ring_next + 1) % self.NRING
        rk = "ring%d" % j
        deps = self._deps(reads, writes)
        if self.ring_n[j] > 0:
            deps.append((rk, 16 * self.ring_n[j]))
        self._emit_waits(eng, deps)
        self.ring_n[j] += 1
        tok = (rk, 16 * self.ring_n[j])
        eng.prog.append(("dma", fn, rk))
        self._record(tok, reads, writes)
        return tok

    def barrier(self):
        toks = [(k, e.count) for k, e in self.engs.items() if e.count > 0]
        toks += [("ring%d" % j, 16 * n) for j, n in enumerate(self.ring_n) if n > 0]
        for e in self.engs.values():
            self._emit_waits(e, toks)

    def flush(self):
        self.barrier()
        nc = self.nc
        sems = self.sems
        with nc.Block() as block:
            def run(eng):
                def body(h):
                    for item in eng.prog:
                        if item[0] == "wait":
                            h.wait_ge(sems[item[1]], item[2])
                        elif item[0] == "op":
                            item[1](h).then_inc(sems[item[2]], 1)
                        else:
                            item[1](h).then_inc(sems[item[2]], 16)
                return body
            block.tensor(run(self.engs["pe"]))
            block.vector(run(self.engs["dve"]))
            block.scalar(run(self.engs["act"]))
            block.gpsimd(run(self.engs["pool"]))
            block.sync(run(self.engs["sp"]))
        for e in self.engs.values():
            e.prog = []


class Rot:
    uid = 0

    def __init__(self, nc, st, name, shape, dt, n=2):
        Rot.uid += 1
        self.bufs = [st.enter_context(nc.sbuf_tensor("rt%d_%s_%d" % (Rot.uid, name, i), list(shape), dt)) for i in range(n)]
        self.regs = [Region("%s_%d" % (name, i)) for i in range(n)]
        self.subregs = [[Region("%s_%d_%d" % (name, i, j)) for j in range(8)] for i in range(n)]
        self.i = 0

    def next2(self):
        i = self.i
        b, r = self.next()
        return b, r, self.subregs[i]

    def next(self):
        b, r = self.bufs[self.i], self.regs[self.i]
        self.i = (self.i + 1) % len(self.bufs)
        return b, r


def build(debug=False, nlayers=DEPTH):
    nc = bass.Bass("TRN2", target_bir_lowering=False)

    def din(name, shape, dt=F32):
        return nc.dram_tensor(name, list(shape), dt, kind="ExternalInput").ap()

    def dscr(name, shape, dt, dbg=False):
        if dbg and debug:
            return nc.dram_tensor(name, list(shape), dt, kind="ExternalOutput").ap()
        return nc.dram_tensor(name, list(shape), dt).ap()

    xcat = din("xcat", [NTOK, D])
    cmod_d = din("cmod", [128, 8, 2])
    w_ada = din("w_ada", [DEPTH, D, 6 * D])
    badaT_d = din("badaT", [DEPTH, 128, 48])
    badaR_d = din("badaR", [DEPTH, 2, 6 * D])
    w_in = din("w_in", [DEPTH, D, 6432])
    qg_d = din("qg64", [DEPTH, 128, 2, 64])
    kg_d = din("kg64", [DEPTH, 128, 2, 64])
    sink_d = din("sinkb", [DEPTH, 128, 8])
    sgcol_d = din("sgcol", [DEPTH, 128, 4])
    wsguT_d = din("wsguT", [DEPTH, 128, 4, 128])
    bsgu_d = din("bsgub", [DEPTH, 128, 4, 128])
    wgate_d = din("wgate", [DEPTH, 2, 16, 256])
    bgrow_d = din("bgrow", [DEPTH, 1, 2, 256])
    glcol_d = din("glcol", [DEPTH, 128, 1])
    w_br = din("w_br", [DEPTH, 3, 512, D])
    w_o = din("w_o", [DEPTH, D, D])
    w_ff1 = din("w_ff1", [DEPTH, D, 4096])
    w_ff2 = din("w_ff2", [DEPTH, 4096, D])
    ropeC_d = din("ropeC", [128, 34, 64])
    ropeS_d = din("ropeS", [128, 34, 64])
    ident_d = din("ident", [128, 128])
    mge_d = din("mask_ge", [128, 128])
    mle_d = din("mask_le", [128, 128])
    sel_d = din("sel", [2, 2, 128])
    out = nc.dram_tensor("out", [NLAT, D], F32, kind="ExternalOutput").ap()

    brT = dscr("brT", [9, 128, 12, 512], BF16, True)
    sig = dscr("sig", [9, 128, 24, 512], BF16, True)
    qT_s = dscr("qT_s", [512, NTOK], BF16, True)
    kT_s = dscr("kT_s", [128, NTOK], BF16, True)
    Vx_s = dscr("Vx_s", [NTOK, 130], BF16, True)
    vn_s = dscr("vn_s", [NTOK, 512], BF16, True)
    qkc_s = dscr("qkc_s", [512, NTOK], F32, True)
    vc_s = dscr("vc_s", [NTOK, 512], BF16, True)
    rs_s = dscr("rs_s", [NTOK, 512], BF16, True)
    zT_s = dscr("zT_s", [32, NTOK], F32, True)
    of_s = dscr("of_s", [NTOK, 512], F32, True)
    ob_s = dscr("ob_s", [NTOK, 512], F32, True)
    xm = dscr("xm", [NTOK, D], F32, True)
    xa = dscr("xa", [NTOK, D], F32, True)

    top = contextlib.ExitStack()
    with top:
        S = Sched(nc, top)

        def MM(out_, lhsT, rhs, start, stop, r, w):
            S.op("pe", lambda h: h.matmul(out_, lhsT=lhsT, rhs=rhs, start=start, stop=stop), r, w)

        def TR(out_, in_, ident, r, w):
            S.op("pe", lambda h: h.transpose(out=out_, in_=in_, identity=ident), r, w)

        def ACTF(out_, in_, func, r, w, scale=None, bias=None, accum=None):
            kw = {}
            if scale is not None:
                kw["scale"] = scale
            if bias is not None:
                kw["bias"] = bias
            if accum is not None:
                kw["accum_out"] = accum
            S.op("act", lambda h: h.activation(out=out_, in_=in_, func=func, **kw), r, w)

        def TT(eng, out_, in0, in1, op, r, w):
            S.op(eng, lambda h: h.tensor_tensor(out=out_, in0=in0, in1=in1, op=op), r, w)

        def TS(eng, out_, in0, s1, op0, r, w, s2=None, op1=None):
            if op1 is None:
                S.op(eng, lambda h: h.tensor_scalar(out=out_, in0=in0, scalar1=s1, scalar2=None, op0=op0), r, w)
            else:
                S.op(eng, lambda h: h.tensor_scalar(out=out_, in0=in0, scalar1=s1, scalar2=s2, op0=op0, op1=op1), r, w)

        def STT(out_, in0, scalar, in1, op0, op1, r, w):
            S.op("dve", lambda h: h.scalar_tensor_tensor(out=out_, in0=in0, scalar=scalar, in1=in1, op0=op0, op1=op1), r, w)

        def CP(eng, out_, in_, r, w):
            if eng == "act":
                ACTF(out_, in_, AF.Copy, r, w)
            else:
                S.op(eng, lambda h: h.tensor_copy(out=out_, in_=in_), r, w)

        def RED(out_, in_, r, w):
            S.op("dve", lambda h: h.tensor_reduce(out=out_, in_=in_, axis=AX.X, op=ALU.add), r, w)

        def RCP(out_, in_, r, w):
            S.op("dve", lambda h: h.reciprocal(out=out_, in_=in_), r, w)

        def MSET(eng, ap, val, w):
            S.op(eng, lambda h: h.memset(ap, val), (), w)

        def DMA(q, out_, in_, r, w):
            S.dma(q, lambda h: h.dma_start(out=out_, in_=in_), r, w)

        def sb(st, name, shape, dt):
            Rot.uid += 1
            t = st.enter_context(nc.sbuf_tensor("sb%d_%s" % (Rot.uid, name), list(shape), dt))
            return t, Region(name)

        pb = [top.enter_context(nc.psum_tensor("pb%d" % i, [128, 512], F32)) for i in range(8)]
        pbr = [Region("pb%d" % i) for i in range(8)]
        bank_ctr = [0]

        BKS = {"all": list(range(8)), "mm": [0, 1, 2, 3], "aux": [4, 5], "tr": [6, 7]}
        bctr = {k: 0 for k in BKS}

        def nbank(kind="all"):
            lst = BKS[kind]
            i = lst[bctr[kind] % len(lst)]
            bctr[kind] += 1
            return pb[i], pbr[i]

        def pipe(gens, depth):
            active = []
            it = iter(gens)
            done = False
            while True:
                if not done and len(active) < depth:
                    try:
                        active.append(next(it))
                    except StopIteration:
                        done = True
                if not active:
                    if done:
                        break
                    continue
                for g_ in list(active):
                    try:
                        next(g_)
                    except StopIteration:
                        active.remove(g_)

        identf, r_identf = sb(top, "identf", [128, 128], F32)
        identb, r_identb = sb(top, "identb", [128, 128], BF16)
        mge, r_mge = sb(top, "mge", [128, 128], BF16)
        mle, r_mle = sb(top, "mle", [128, 128], BF16)
        sel, r_sel = sb(top, "sel", [2, 2, 128], F32)
        cmod, r_cmod = sb(top, "cmod", [128, 8, 2], F32)
        scb, r_scb = sb(top, "scb", [128, 8, 2], F32)
        scbb, r_scbb = sb(top, "scbb", [128, 8, 2], BF16)
        modT, r_modT = sb(top, "modT", [128, 48, 2], F32)
        gbc, r_gbc = sb(top, "gbc", [128, 2, 2, 1024], F32)
        r_const = Region("const")
        negh, r_negh = sb(top, "negh", [128, 8], F32)
        MSET("dve", negh[:], -0.5, [r_negh])

        DMA("sp", identf[:], ident_d, [], [r_identf])
        DMA("sp", sel[:], sel_d, [], [r_sel])
        DMA("sp", cmod[:], cmod_d, [], [r_cmod])
        DMA("pool", mge[:], mge_d, [], [r_mge])
        DMA("pool", mle[:], mle_d, [], [r_mle])
        CP("dve", identb[:], identf[:], [r_identf], [r_identb])
        ACTF(scb[:], cmod[:], AF.Silu, [r_cmod], [r_scb])
        CP("dve", scbb[:], scb[:], [r_scb], [r_scbb])
        S.flush()

        def frontA(st_bufs, src, tok0, ntok, xt, r_xt):
            junk, r_junk, ss, r_ss, xn, r_xn = st_bufs
            if r_junk is None:
                r_junk = Region("junk")
            nsub = ntok // 128
            DMA("sp", xt[:, 0:nsub, :], src[tok0:tok0 + ntok, :].rearrange("(s p) d -> p s d", p=128), [], [r_xt])
            for sub in range(nsub):
                ACTF(junk[:], xt[:, sub, :], AF.Square, [r_xt], [r_junk, r_ss], accum=ss[:, sub:sub + 1])
            TS("pool", ss[:, 4:4 + nsub], ss[:, 0:nsub], 1.0 / D, ALU.mult, [r_ss], [r_ss], s2=EPS, op1=ALU.add)
            TT("pool", ss[:, 8:8 + nsub], ss[:, 4:4 + nsub], negh[:, 0:nsub], ALU.pow, [r_negh], [r_ss])
            for sub in range(nsub):
                if sub % 2 == 0:
                    TS("dve", xn[:, sub, :], xt[:, sub, :], ss[:, 8 + sub:9 + sub], ALU.mult, [r_xt, r_ss], [r_xn])
                else:
                    ACTF(xn[:, sub, :], xt[:, sub, :], AF.Identity, [r_xt, r_ss], [r_xn], scale=ss[:, 8 + sub:9 + sub])

        def frontB(st_bufs, ntok, sh_j, sc_j, s, dst, r_dst):
            junk, r_junk, ss, r_ss, xn, r_xn = st_bufs
            nsub = ntok // 128
            for kc in range(8):
                bk, rb = nbank("tr")
                bv = bk[:].bitcast(BF16)
                for sub in range(nsub):
                    TR(bv[:, sub * 128:(sub + 1) * 128], xn[:, sub, kc * 128:(kc + 1) * 128], identb[:],
                       [r_xn, r_identb], [rb])
                if kc % 2 == 0:
                    TS("dve", dst[:, kc, 0:ntok], bv[:, 0:ntok], modT[:, sc_j + kc, s:s + 1], ALU.mult,
                       [r_modT], [rb, r_dst], s2=modT[:, sh_j + kc, s:s + 1], op1=ALU.add)
                else:
                    ACTF(dst[:, kc, 0:ntok], bv[:, 0:ntok], AF.Identity, [r_modT], [rb, r_dst],
                         scale=modT[:, sc_j + kc, s:s + 1], bias=modT[:, sh_j + kc, s:s + 1])

        def grp_norm(wk, src, r_src, H, Dd, gain, r_gain, out_, r_out):
            sq, r_sq = wk["sq"].next()
            st4, r_st4 = wk["st"].next()
            tt, r_tt = wk["t"].next()
            W = H * Dd
            TT("pool", sq[:, 0:W], src, src, ALU.mult, [r_src], [r_sq])
            RED(st4[:, 0:H], sq[:, 0:W].rearrange("p (h d) -> p h d", d=Dd), [r_sq], [r_st4])
            ACTF(st4[:, 8:8 + H], st4[:, 0:H], AF.Sqrt, [r_st4], [r_st4], scale=1.0 / Dd, bias=EPS)
            RCP(st4[:, 16:16 + H], st4[:, 8:8 + H], [r_st4], [r_st4])
            TT("dve", tt[:, 0:W].rearrange("p (h d) -> p h d", d=Dd), src.rearrange("p (h d) -> p h d", d=Dd),
               st4[:, 16:16 + H].unsqueeze(2).to_broadcast([128, H, Dd]), ALU.mult, [r_src, r_st4], [r_tt])
            TT("pool", out_, tt[:, 0:W], gain, ALU.mult, [r_tt, r_gain], [r_out])

        def grp_norm_g(wk, src, r_src, H, Dd, gain, r_gain, out_, r_out):
            sq, r_sq = wk["sq"].next()
            st4, r_st4 = wk["st"].next()
            tt, r_tt = wk["t"].next()
            W = H * Dd
            TT("pool", sq[:, 0:W], src, src, ALU.mult, [r_src], [r_sq])
            yield
            RED(st4[:, 0:H], sq[:, 0:W].rearrange("p (h d) -> p h d", d=Dd), [r_sq], [r_st4])
            yield
            ACTF(st4[:, 8:8 + H], st4[:, 0:H], AF.Sqrt, [r_st4], [r_st4], scale=1.0 / Dd, bias=EPS)
            yield
            RCP(st4[:, 16:16 + H], st4[:, 8:8 + H], [r_st4], [r_st4])
            yield
            TT("dve", tt[:, 0:W].rearrange("p (h d) -> p h d", d=Dd), src.rearrange("p (h d) -> p h d", d=Dd),
               st4[:, 16:16 + H].unsqueeze(2).to_broadcast([128, H, Dd]), ALU.mult, [r_src, r_st4], [r_tt])
            yield
            TT("pool", out_, tt[:, 0:W], gain, ALU.mult, [r_tt, r_gain], [r_out])

        def rstd_g(wk, src, r_src, H, Dd, use_act, pool_pow=False):
            st4, r_st4 = wk["st"].next()
            W = H * Dd
            sq, r_sq = wk["sq"].next()
            if use_act:
                for h_ in range(H):
                    ACTF(sq[:, h_ * Dd:(h_ + 1) * Dd], src[:, h_ * Dd:(h_ + 1) * Dd], AF.Square, [r_src], [r_sq, r_st4],
                         accum=st4[:, h_:h_ + 1])
                yield None
            else:
                TT("pool", sq[:, 0:W], src, src, ALU.mult, [r_src], [r_sq])
                yield None
                RED(st4[:, 0:H], sq[:, 0:W].rearrange("p (h d) -> p h d", d=Dd), [r_sq], [r_st4])
                yield None
            if pool_pow:
                TS("pool", st4[:, 8:8 + H], st4[:, 0:H], 1.0 / Dd, ALU.mult, [r_st4], [r_st4], s2=EPS, op1=ALU.add)
                yield None
                TT("pool", st4[:, 16:16 + H], st4[:, 8:8 + H], negh[:, 0:H], ALU.pow, [r_negh], [r_st4])
            else:
                ACTF(st4[:, 8:8 + H], st4[:, 0:H], AF.Sqrt, [r_st4], [r_st4], scale=1.0 / Dd, bias=EPS)
                yield None
                RCP(st4[:, 16:16 + H], st4[:, 8:8 + H], [r_st4], [r_st4])
            yield (st4, r_st4)

        def norm_wk(st, pfx, n=2):
            return {"sq": Rot(nc, st, pfx + "sq", [128, 512], F32, n=n), "st": Rot(nc, st, pfx + "st", [128, 24], F32, n=n)}

        for l in range(nlayers):
            last = (l == DEPTH - 1)
            src_x = xcat if l == 0 else xa

            with contextlib.ExitStack() as st:
                wa = Rot(nc, st, "wa", [128, 8, 512], F32, n=6)
                badaT, r_badaT = sb(st, "badaT", [128, 48], F32)
                badaR, r_badaR = sb(st, "badaR", [2, 6 * D], F32)
                grow = Rot(nc, st, "grow", [2, 512], F32, n=3)
                DMA("sp", badaT[:], badaT_d[l], [], [r_badaT])
                DMA("sp", badaR[:], badaR_d[l], [], [r_badaR])
                wabs = Rot(nc, st, "wab", [128, 8, 512], BF16, n=3)
                cast_eng = ["pool", "dve", "act"]
                for cg in range(12):
                    wt32, rw32 = wa.next()
                    DMA("sp", wt32[:], w_ada[l][:, cg * 512:(cg + 1) * 512].rearrange("(kc p) n -> p kc n", p=128),
                        [], [rw32])
                    wt, rw = wabs.next()
                    CP(cast_eng[cg % 3], wt[:], wt32[:], [rw32], [rw])
                    which = cg // 2
                    if which in (2, 5):
                        bk, rb = nbank()
                        for kc in range(8):
                            MM(bk[0:2, :], scbb[:, kc, :], wt[:, kc, :], kc == 0, kc == 7, [r_scbb, rw], [rb])
                        gr, rg = grow.next()
                        TT("dve", gr[:], bk[0:2, :], badaR[:, cg * 512:(cg + 1) * 512], ALU.add, [r_badaR], [rb, rg])
                        for s in range(2):
                            bk2, rb2 = nbank()
                            MM(bk2[:], sel[:, s, :], gr[:], True, True, [r_sel, rg], [rb2])
                            CP("act", gbc[:, 0 if which == 2 else 1, s, (cg % 2) * 512:(cg % 2 + 1) * 512], bk2[:],
                               [], [rb2, r_gbc])
                    else:
                        bk, rb = nbank()
                        for kc in range(8):
                            MM(bk[0:2, :], scbb[:, kc, :], wt[:, kc, :], kc == 0, kc == 7, [r_scbb, rw], [rb])
                        gr, rg = grow.next()
                        TT("dve", gr[:], bk[0:2, :], badaR[:, cg * 512:(cg + 1) * 512], ALU.add, [r_badaR], [rb, rg])
                        bk2, rb2 = nbank()
                        for f in range(4):
                            TR(bk2[:, f * 2:f * 2 + 2], gr[0:2, f * 128:(f + 1) * 128], identf[0:2, 0:2], [rg, r_identf], [rb2])
                        CP("act", modT[:, cg * 4:cg * 4 + 4, :], bk2[:, 0:8].rearrange("p (f s) -> p f s", s=2), [], [rb2, r_modT])
                TS("dve", modT[:, 8:16, :], modT[:, 8:16, :], 1.0, ALU.add, [r_modT], [r_modT])
                TS("dve", modT[:, 32:40, :], modT[:, 32:40, :], 1.0, ALU.add, [r_modT], [r_modT])
                S.flush()

            with contextlib.ExitStack() as stP:
                hT, r_hT = sb(stP, "hT_all", [128, 8, NTOK], BF16)
                wg = Rot(nc, stP, "wg", [128, 8, 512], BF16)
                with contextlib.ExitStack() as st:
                    junk, r_junk = sb(st, "f_junk", [128, D], BF16)
                    sss = Rot(nc, st, "f_ss", [128, 12], F32, n=3)
                    xns = Rot(nc, st, "f_xn", [128, 4, D], BF16, n=3)
                    xts = Rot(nc, st, "f_xt", [128, 4, D], F32, n=3)
                    hregs = [Region("hT%d" % i_) for i_ in range(len(TG))]

                    def front_g(gi_, tok0, ntok):
                        xt, r_xt = xts.next()
                        ss, r_ss = sss.next()
                        xn, r_xn = xns.next()
                        bufs = (junk, Region("junk"), ss, r_ss, xn, r_xn)
                        frontA(bufs, src_x, tok0, ntok, xt, r_xt)
                        yield
                        frontB(bufs, ntok, 0, 8, 1 if tok0 == 0 else 0, hT[:, :, tok0:tok0 + ntok], hregs[gi_])
                    pipe((front_g(gi_, t0_, n_) for gi_, (t0_, n_) in enumerate(TG)), 3)
                    S.flush()
                    r_hT = Region("hT_ro")

                groups = [("v", 512, 512), ("u", 0, 512), ("qb", 1024, 512), ("kv", 1536, 256), ("qkc", 1792, 512),
                          ("vc", 2304, 512), ("rc", 2816, 512), ("z", 3328, 32)]

                def load_wg(gi):
                    name, c0, ncol = groups[gi]
                    wt, rw = wg.next()
                    DMA("pool", wt[:, :, 0:ncol], w_in[l][:, c0:c0 + ncol].rearrange("(kc p) n -> p kc n", p=128), [], [rw])
                    return wt, rw

                def proj_tm(wt, rw, ncol, tcol0, kind="mm"):
                    bk, rb = nbank(kind)
                    for kc in range(8):
                        MM(bk[:, 0:ncol], hT[:, kc, tcol0:tcol0 + 128], wt[:, kc, 0:ncol], kc == 0, kc == 7, [r_hT, rw], [rb])
                    return bk, rb

                def proj_fm(wt, rw, c0, m, tok0, ntok, kind="mm"):
                    bk, rb = nbank(kind)
                    for kc in range(8):
                        MM(bk[0:m, 0:ntok], wt[:, kc, c0:c0 + m], hT[:, kc, tok0:tok0 + ntok], kc == 0, kc == 7, [r_hT, rw], [rb])
                    return bk, rb

                def rope_g(wk, src, r_src, H, Ct, St, r_tab, out_, r_out):
                    t1, r_t1 = wk["t1"].next()
                    t2, r_t2 = wk["t2"].next()
                    W = H * 64
                    TT("dve", t1[:, 0:W].rearrange("p (h d) -> p h d", d=64), src.rearrange("p (h d) -> p h d", d=64),
                       Ct.unsqueeze(1).to_broadcast([128, H, 64]), ALU.mult, [r_src, r_tab], [r_t1])
                    sv = src.rearrange("p (h a b c) -> p h a b c", a=2, b=2, c=16)
                    tv = t2[:, 0:W].rearrange("p (h a b c) -> p h a b c", a=2, b=2, c=16)
                    Sv = St.rearrange("p (a b c) -> p a b c", a=2, b=2, c=16)
                    for j in range(2):
                        TT("pool", tv[:, :, :, j, :], sv[:, :, :, 1 - j, :],
                           Sv[:, :, j, :].unsqueeze(1).to_broadcast([128, H, 2, 16]), ALU.mult, [r_src, r_tab], [r_t2])
                    yield
                    TT("dve", out_, t1[:, 0:W], t2[:, 0:W], ALU.add, [r_t1, r_t2], [r_out])

                nxt = load_wg(0)
                SIMPLE = ("qkc", "vc", "rc", "z")
                shared_st = contextlib.ExitStack()

                @contextlib.contextmanager
                def group_scope(gname_):
                    if gname_ in SIMPLE:
                        yield shared_st
                    else:
                        with contextlib.ExitStack() as st_:
                            yield st_

                for gi, (gname, gc0, gncol) in enumerate(groups):
                    wt, rw = nxt
                    if gi + 1 < len(groups):
                        nxt = load_wg(gi + 1)
                    with group_scope(gname) as st:
                        if gname == "v":
                            ND = 8
                            wk = norm_wk(st, "v_", n=ND)
                            gvs = Rot(nc, st, "v_gv", [128, 512], F32, n=ND)
                            vns = Rot(nc, st, "v_vn", [128, 4, 512], BF16, n=3)

                            def v_sub(tok0, ntok, sub, grp):
                                if sub == 0:
                                    grp["vn"], _, grp["regs"] = vns.next2()
                                    grp["left"] = ntok // 128
                                vn = grp["vn"]
                                r_vn = grp["regs"][sub]
                                bk, rb = proj_tm(wt, rw, 512, tok0 + sub * 128)
                                gv, r_gv = gvs.next()
                                ACTF(gv[:], bk[:], AF.Gelu, [], [rb, r_gv])
                                yield
                                res = None
                                for res in rstd_g(wk, gv[:], r_gv, 4, 128, False, pool_pow=True):
                                    yield
                                st4, r_st4 = res
                                TT("dve", vn[:, sub, :].rearrange("p (h d) -> p h d", d=128), gv[:].rearrange("p (h d) -> p h d", d=128),
                                   st4[:, 16:20].unsqueeze(2).to_broadcast([128, 4, 128]), ALU.mult, [r_gv, r_st4], [r_vn])
                                grp["left"] -= 1
                                if grp["left"] == 0:
                                    DMA("sp", vn_s[tok0:tok0 + ntok, :].rearrange("(s p) f -> p s f", p=128),
                                        vn[:, 0:ntok // 128, :], grp["regs"][0:ntok // 128], [])

                            def v_all():
                                for (tok0, ntok) in TG:
                                    grp = {}
                                    for sub in range(ntok // 128):
                                        yield v_sub(tok0, ntok, sub, grp)
                            pipe(v_all(), ND)
                        elif gname == "u":
                            wsg32, r_wsg32 = sb(st, "wsg32", [128, 4, 128], F32)
                            wsg, r_wsg = sb(st, "wsg", [128, 4, 128], BF16)
                            bsg, r_bsg = sb(st, "bsg", [128, 4, 128], F32)
                            sgcol, r_sgcol = sb(st, "sgcol", [128, 4], F32)
                            DMA("sp", wsg32[:], wsguT_d[l], [], [r_wsg32])
                            DMA("sp", bsg[:], bsgu_d[l], [], [r_bsg])
                            DMA("sp", sgcol[:], sgcol_d[l], [], [r_sgcol])
                            CP("dve", wsg[:], wsg32[:], [r_wsg32], [r_wsg])
                            vns = Rot(nc, st, "u_vn", [128, 4, 512], BF16, n=3)
                            gus = Rot(nc, st, "u_gu", [128, 512], BF16, n=4)
                            tms = Rot(nc, st, "u_tm", [128, 512], F32, n=4)
                            aTs = Rot(nc, st, "u_aT", [128, 4, 512], BF16, n=3)

                            u_pref = {}
                            u_next = {TG[i_][0]: TG[i_ + 1] for i_ in range(len(TG) - 1)}

                            def u_load(tok0, ntok):
                                vn_, r_vn_ = vns.next()
                                DMA("sp", vn_[:, 0:ntok // 128, :], vn_s[tok0:tok0 + ntok, :].rearrange("(s p) f -> p s f", p=128),
                                    [], [r_vn_])
                                u_pref[tok0] = (vn_, r_vn_)

                            def u_g(tok0, ntok, g, grp):
                                nsub = ntok // 128
                                if g == 0:
                                    if tok0 not in u_pref:
                                        u_load(tok0, ntok)
                                    grp["vn"], grp["r_vn"] = u_pref.pop(tok0)
                                    nx_ = u_next.get(tok0)
                                    if nx_ is not None:
                                        u_load(*nx_)
                                    grp["aT"], _, grp["regs"] = aTs.next2()
                                    grp["regs"] = grp["regs"][0:4]
                                    grp["left"] = 4
                                vn, r_vn, aT = grp["vn"], grp["r_vn"], grp["aT"]
                                bk, rb = proj_fm(wt, rw, g * 128, 128, tok0, ntok)
                                gu, r_gu = gus.next()
                                ACTF(gu[:, 0:ntok], bk[:, 0:ntok], AF.Gelu, [], [rb, r_gu])
                                bk2, rb2 = nbank("aux")
                                for sub in range(nsub):
                                    MM(bk2[:, sub * 128:(sub + 1) * 128], vn[:, sub, g * 128:(g + 1) * 128], wsg[:, g, :],
                                       True, True, [r_vn, r_wsg], [rb2])
                                yield
                                tm, r_tm = tms.next()
                                STT(tm[:, 0:ntok].rearrange("p (s q) -> p s q", q=128),
                                    bk2[:, 0:ntok].rearrange("p (s q) -> p s q", q=128), sgcol[:, g:g + 1],
                                    bsg[:, g, :].unsqueeze(1).to_broadcast([128, nsub, 128]), ALU.mult, ALU.add,
                                    [r_bsg, r_sgcol], [rb2, r_tm])
                                yield
                                TT("pool", aT[:, g, 0:ntok], tm[:, 0:ntok], gu[:, 0:ntok], ALU.mult, [r_tm, r_gu], [grp["regs"][g]])
                                grp["left"] -= 1
                                if grp["left"] == 0:
                                    DMA("sp", brT[GI[tok0], :, 0:4, 0:ntok],
                                        aT[:, :, 0:ntok], grp["regs"], [])

                            def u_all():
                                for (tok0, ntok) in TG:
                                    grp = {}
                                    for g in range(4):
                                        yield u_g(tok0, ntok, g, grp)
                            pipe(u_all(), 4)
                        elif gname in ("qb", "kv"):
                            isq = gname == "qb"
                            H = 8 if isq else 2
                            W = H * 64
                            ND = 7
                            wk = norm_wk(st, gname + "_", n=ND)
                            t1s = Rot(nc, st, gname + "_t1", [128, 512], F32, n=ND)
                            t2s = Rot(nc, st, gname + "_t2", [128, 512], F32, n=ND)
                            raws = Rot(nc, st, gname + "_raw", [128, 512], F32, n=ND)
                            qrs = Rot(nc, st, gname + "_qr", [128, 512], BF16, n=3)
                            qTgs = Rot(nc, st, gname + "_Tg", [64, H, 512], BF16, n=3)
                            gain, r_gain = sb(st, gname + "_gain", [128, 2, 64], F32)
                            DMA("sp", gain[:], (qg_d if isq else kg_d)[l], [], [r_gain])
                            tabA, r_tabA = sb(st, gname + "_tabA", [128, 2, 34, 64], F32)
                            DMA("sp", tabA[:, 0, :, :], ropeC_d, [], [r_tabA])
                            DMA("sp", tabA[:, 1, :, :], ropeS_d, [], [r_tabA])
                            for j_ in range(2):
                                TT("dve" if j_ else "pool", tabA[:, j_, :, :], tabA[:, j_, :, :],
                                   gain[:, j_, :].unsqueeze(1).to_broadcast([128, 34, 64]), ALU.mult, [r_gain], [r_tabA])
                            if not isq:
                                vxs = Rot(nc, st, "kv_vx", [128, 4, 2, 65], BF16, n=3)
                                for bi_ in range(len(vxs.bufs)):
                                    MSET("pool", vxs.bufs[bi_][:], 1.0, vxs.subregs[bi_])
                            dstT = (qT_s if isq else kT_s)

                            def qk_sub(tok0, ntok, sub, grp):
                                nsub = ntok // 128
                                ti0 = tok0 // 128
                                if sub == 0:
                                    grp["qTg"], _, grp["regs"] = qTgs.next2()
                                    grp["left"] = nsub
                                    if not isq:
                                        grp["vx"], r_vx_main, grp["vregs"] = vxs.next2()
                                qTg = grp["qTg"]
                                r_tab = r_tabA
                                tabC = tabA[:, 0, ti0 + sub, :]
                                tabS = tabA[:, 1, ti0 + sub, :]
                                bk, rb = proj_tm(wt, rw, gncol, tok0 + sub * 128)
                                raw, r_raw = raws.next()
                                CP("act", raw[:, 0:W], bk[:, 0:W], [], [rb, r_raw])
                                if not isq:
                                    CP("act", grp["vx"][:, sub, :, 0:64], bk[:, 128:256].rearrange("p (h d) -> p h d", d=64),
                                       [], [rb, grp["vregs"][sub]])
                                yield
                                src = raw[:, 0:W]
                                t1, r_t1 = t1s.next()
                                t2, r_t2 = t2s.next()
                                TT("dve", t1[:, 0:W].rearrange("p (h d) -> p h d", d=64), src.rearrange("p (h d) -> p h d", d=64),
                                   tabC.unsqueeze(1).to_broadcast([128, H, 64]), ALU.mult, [r_raw, r_tab], [r_t1])
                                sv = src.rearrange("p (h a b c) -> p h a b c", a=2, b=2, c=16)
                                tv = t2[:, 0:W].rearrange("p (h a b c) -> p h a b c", a=2, b=2, c=16)
                                Sv = tabS.rearrange("p (a b c) -> p a b c", a=2, b=2, c=16)
                                for j in range(2):
                                    TT("pool", tv[:, :, :, j, :], sv[:, :, :, 1 - j, :],
                                       Sv[:, :, j, :].unsqueeze(1).to_broadcast([128, H, 2, 16]), ALU.mult, [r_raw, r_tab], [r_t2])
                                res = None
                                for res in rstd_g(wk, src, r_raw, H, 64, False):
                                    yield
                                st4, r_st4 = res
                                TT("dve", t1[:, 0:W], t1[:, 0:W], t2[:, 0:W], ALU.add, [r_t2], [r_t1])
                                yield
                                qr, r_qr = qrs.next()
                                TT("dve", qr[:, 0:W].rearrange("p (h d) -> p h d", d=64), t1[:, 0:W].rearrange("p (h d) -> p h d", d=64),
                                   st4[:, 16:16 + H].unsqueeze(2).to_broadcast([128, H, 64]), ALU.mult, [r_t1, r_st4], [r_qr])
                                yield
                                bk2, rb2 = nbank("tr")
                                bv = bk2[:].bitcast(BF16)
                                for h_ in range(H):
                                    TR(bv[0:64, h_ * 128:(h_ + 1) * 128], qr[:, h_ * 64:(h_ + 1) * 64], identb[:],
                                       [r_qr, r_identb], [rb2])
                                yield
                                CP("act", qTg[:, :, sub * 128:(sub + 1) * 128],
                                   bv[0:64, 0:H * 128].rearrange("p (h t) -> p h t", t=128), [], [rb2, grp["regs"][sub]])
                                grp["left"] -= 1
                                if grp["left"] == 0:
                                    DMA("sp", dstT[:, tok0:tok0 + ntok].rearrange("(h d) t -> d h t", d=64), qTg[:, :, 0:ntok],
                                        grp["regs"][0:nsub], [])
                                    if not isq:
                                        DMA("sp", Vx_s[tok0:tok0 + ntok, :].rearrange("(s p) f -> p s f", p=128),
                                            grp["vx"][:, 0:nsub, :, :].rearrange("p s h d -> p s (h d)"), grp["vregs"][0:nsub], [])

                            def qk_all():
                                for (tok0, ntok) in TG:
                                    grp = {}
                                    for sub in range(ntok // 128):
                                        yield qk_sub(tok0, ntok, sub, grp)
                            pipe(qk_all(), ND)
                        elif gname == "qkc":
                            bufs = Rot(nc, st, "qkc_b", [128, 4, 512], F32)
                            for (tok0, ntok) in TG:
                                bf, r_bf = bufs.next()
                                for f in range(4):
                                    bk, rb = proj_fm(wt, rw, f * 128, 128, tok0, ntok)
                                    CP("act" if f % 2 else "dve", bf[:, f, 0:ntok], bk[:, 0:ntok], [], [rb, r_bf])
                                DMA("sp", qkc_s[:, tok0:tok0 + ntok].rearrange("(f p) t -> p f t", p=128), bf[:, :, 0:ntok],
                                    [r_bf], [])
                        elif gname in ("vc", "rc"):
                            bufs = Rot(nc, st, gname + "_b", [128, 4, 512], BF16)
                            dst = vc_s if gname == "vc" else rs_s
                            for (tok0, ntok) in TG:
                                nsub = ntok // 128
                                bf, r_bf = bufs.next()
                                for sub in range(nsub):
                                    bk, rb = proj_tm(wt, rw, 512, tok0 + sub * 128)
                                    if gname == "vc":
                                        CP("act" if sub % 2 else "dve", bf[:, sub, :], bk[:], [], [rb, r_bf])
                                    else:
                                        ACTF(bf[:, sub, :], bk[:], AF.Silu, [], [rb, r_bf])
                                DMA("sp", dst[tok0:tok0 + ntok, :].rearrange("(s p) f -> p s f", p=128), bf[:, 0:nsub, :],
                                    [r_bf], [])
                        elif gname == "z":
                            bufs = Rot(nc, st, "z_b", [16, 2, 512], F32)
                            for (tok0, ntok) in TG:
                                bf, r_bf = bufs.next()
                                for dn in range(2):
                                    bk, rb = proj_fm(wt, rw, dn * 16, 16, tok0, ntok)
                                    CP("dve", bf[:, dn, 0:ntok], bk[0:16, 0:ntok], [], [rb, r_bf])
                                DMA("sp", zT_s[:, tok0:tok0 + ntok].rearrange("(a z) t -> z a t", z=16), bf[:, :, 0:ntok],
                                    [r_bf], [])
                        else:
                            gidx = int(gname[1:])
                            bufs = Rot(nc, st, "g_b", [128, 4, 512], BF16)
                            for (tok0, ntok) in TG:
                                if last and tok0 == 0:
                                    continue
                                bf, r_bf = bufs.next()
                                for f in range(4):
                                    bk, rb = proj_fm(wt, rw, f * 128, 128, tok0, ntok)
                                    ACTF(bf[:, f, 0:ntok], bk[:, 0:ntok], AF.Sigmoid, [], [rb, r_bf])
                                DMA("sp", sig[GI[tok0], :, gidx * 4:(gidx + 1) * 4, 0:ntok],
                                    bf[:, :, 0:ntok], [r_bf], [])
                        if gname not in SIMPLE or gi == len(groups) - 1:
                            S.flush()
                shared_st.close()

            with contextlib.ExitStack() as st:
                wgt, r_wgt = sb(st, "wgt", [17, 2, 256], F32)
                rmask, r_rmask = sb(st, "rmask", [128, 512], F32)
                DMA("sp", wgt[0:16, :, :], wgate_d[l].rearrange("a z n -> z a n"), [], [r_wgt])
                DMA("sp", wgt[16:17, :, :], bgrow_d[l], [], [r_wgt])
                MSET("pool", rmask[:], 1.0, [r_rmask])
                MSET("pool", rmask[:].rearrange("p (c l) -> p c l", l=128)[:, :, 0:1], 0.0, [r_rmask])

                def gla_dir(dn):
                    P = "c%d_" % dn
                    zts = Rot(nc, st, P + "zt", [17, 256], F32, n=2)
                    for bi_ in range(2):
                        MSET("dve", zts.bufs[bi_][:], 1.0, [zts.regs[bi_]])
                    qks = Rot(nc, st, P + "qk", [128, 4, 256], F32, n=2)
                    vchs = Rot(nc, st, P + "vch", [128, 2, 512], BF16, n=3)
                    ex_, r_ex = sb(st, P + "ex", [128, 512], F32)
                    sp_, r_sp = sb(st, P + "sp", [128, 512], F32)
                    cs_, r_cs = sb(st, P + "cs", [128, 512], F32)
                    d1_, r_d1 = sb(st, P + "d1", [128, 512], F32)
                    d2_, r_d2 = sb(st, P + "d2", [128, 512], F32)
                    E = [sb(st, P + "E%d" % i_, [128, 512], F32) for i_ in range(4)]
                    decs = Rot(nc, st, P + "dec", [128, 2, 2], F32, n=2)
                    prods = Rot(nc, st, P + "prod", [128, 4, 2, 256], BF16, n=2)
                    sTs = Rot(nc, st, P + "sT", [128, 4, 128], BF16, n=2)
                    kots = Rot(nc, st, P + "kot", [128, 4, 64], BF16, n=2)
                    obufs = Rot(nc, st, P + "ob", [128, 2, 512], F32, n=2)
                    stt, r_stt = sb(st, P + "state", [128, 2, 128], F32)
                    stb, r_stb = sb(st, P + "stateb", [128, 2, 128], BF16)
                    MSET("dve", stt[:], 0.0, [r_stt])
                    MSET("dve", stb[:], 0.0, [r_stb])
                    mk, r_mk = (mle, r_mle) if dn == 0 else (mge, r_mge)
                    o_dst = of_s if dn == 0 else ob_s
                    order = TG256 if dn == 0 else [TG256[0]] + TG256[:0:-1]
                    iref = 64 if dn == 0 else 63
                    itot = 127 if dn == 0 else 0
                    loaded = {}

                    def load(gi_):
                        tok0, ntok = order[gi_]
                        zt, r_zt = zts.next()
                        qk, r_qk = qks.next()
                        vch, r_vch = vchs.next()
                        DMA("sp", zt[0:16, :], zT_s[dn * 16:(dn + 1) * 16, tok0:tok0 + ntok], [], [r_zt])
                        DMA("sp", qk[:], qkc_s[:, tok0:tok0 + ntok].rearrange("(f p) t -> p f t", p=128), [], [r_qk])
                        DMA("sp", vch[:], vc_s[tok0:tok0 + ntok, :].rearrange("(s p) f -> p s f", p=128), [], [r_vch])
                        loaded[gi_] = (zt, r_zt, qk, r_qk, vch, r_vch)

                    load(0)
                    state = {}

                    def prep(gi_):
                        zt, r_zt, qk, r_qk, vch, r_vch = loaded[gi_]
                        if gi_ + 1 < len(order):
                            load(gi_ + 1)
                        bk, rb = nbank("aux")
                        for pr in range(2):
                            MM(bk[:, pr * 256:(pr + 1) * 256], wgt[:, dn, pr * 128:(pr + 1) * 128], zt[:, :], True, True,
                               [r_wgt, r_zt], [rb])
                        ACTF(ex_[:], bk[:], AF.Exp, [], [rb, r_ex], scale=-1.0)
                        yield
                        ACTF(sp_[:], ex_[:], AF.Ln, [r_ex], [r_sp], bias=1.0)
                        yield
                        S.op("dve", lambda h: h.tensor_tensor_scan(out=cs_[:], data0=rmask[:], data1=sp_[:], initial=0.0,
                                                                   op0=ALU.mult, op1=ALU.add), [r_rmask, r_sp], [r_cs])
                        yield
                        csv = cs_[:].rearrange("p (c l) -> p c l", l=128)
                        cum, r_cum = cs_, r_cs
                        if dn == 1:
                            TT("pool", d1_[:], sp_[:], cs_[:], ALU.subtract, [r_sp, r_cs], [r_d1])
                            yield
                            TT("dve", sp_[:].rearrange("p (c l) -> p c l", l=128), d1_[:].rearrange("p (c l) -> p c l", l=128),
                               csv[:, :, 127:128].to_broadcast([128, 4, 128]), ALU.add, [r_d1, r_cs], [r_sp])
                            yield
                            csv = sp_[:].rearrange("p (c l) -> p c l", l=128)
                            cum, r_cum = sp_, r_sp
                        d1v = d1_[:].rearrange("p (c l) -> p c l", l=128)
                        d2v = d2_[:].rearrange("p (c l) -> p c l", l=128)
                        TT("dve", d1v, csv, csv[:, :, iref:iref + 1].to_broadcast([128, 4, 128]), ALU.subtract, [r_cum], [r_d1])
                        TT("pool", d2v, csv, csv[:, :, itot:itot + 1].to_broadcast([128, 4, 128]), ALU.subtract, [r_cum], [r_d2])
                        ACTF(E[2][0][:], cum[:], AF.Exp, [r_cum], [E[2][1]], scale=-1.0 / 16)
                        yield
                        prod, r_prod = prods.next()
                        dec, r_dec = decs.next()
                        ACTF(E[0][0][:], d1_[:], AF.Exp, [r_d1], [E[0][1]], scale=-1.0 / 16)
                        qv = qk[:, 0:2, :]
                        kv_ = qk[:, 2:4, :]
                        ev = lambda i_: E[i_][0][:].rearrange("p (r t) -> p r t", t=256)
                        STT(prod[:, 2, :, :], qv, 0.125, ev(2), ALU.mult, ALU.mult, [r_qk, E[2][1]], [r_prod])
                        yield
                        ACTF(E[1][0][:], d1_[:], AF.Exp, [r_d1], [E[1][1]], scale=1.0 / 16)
                        STT(prod[:, 0, :, :], qv, 0.125, ev(0), ALU.mult, ALU.mult, [r_qk, E[0][1]], [r_prod])
                        yield
                        ACTF(E[3][0][:], d2_[:], AF.Exp, [r_d2], [E[3][1]], scale=1.0 / 16)
                        TT("pool", prod[:, 1, :, :], kv_, ev(1), ALU.mult, [r_qk, E[1][1]], [r_prod])
                        yield
                        ACTF(dec[:], csv[:, :, itot].rearrange("p (r c) -> p r c", c=2), AF.Exp, [r_cum], [r_dec], scale=-1.0 / 16)
                        TT("pool", prod[:, 3, :, :], kv_, ev(3), ALU.mult, [r_qk, E[3][1]], [r_prod])
                        state[gi_] = (prod, r_prod, dec, r_dec, vch, r_vch)

                    def chunks(gi_):
                        tok0, ntok = order[gi_]
                        prod, r_prod, dec, r_dec, vch, r_vch = state.pop(gi_)
                        obuf, r_ob = obufs.next()
                        for ch in ((0, 1) if dn == 0 else (1, 0)):
                            c0 = ch * 128
                            bks = [nbank("mm"), nbank("mm")]
                            bkts = [nbank("mm"), nbank("mm")]
                            for h_ in range(4):
                                pr, hp = h_ // 2, h_ % 2
                                ps_ = slice(hp * 64, (hp + 1) * 64)
                                MM(bks[hp][0][:, pr * 128:(pr + 1) * 128], prod[ps_, 1, pr, c0:c0 + 128], prod[ps_, 0, pr, c0:c0 + 128],
                                   True, True, [r_prod], [bks[hp][1]])
                            for h_ in range(4):
                                pr, hp = h_ // 2, h_ % 2
                                ps_ = slice(hp * 64, (hp + 1) * 64)
                                TR(bkts[hp][0][:].bitcast(BF16)[:, pr * 64:(pr + 1) * 64], prod[ps_, 3, pr, c0:c0 + 128],
                                   identb[ps_, hp * 64:(hp + 1) * 64], [r_prod, r_identb], [bkts[hp][1]])
                            yield
                            sT, r_sT = sTs.next()
                            kot, r_kot = kots.next()
                            sTv = sT[:].rearrange("p (r q) l -> p r q l", q=2)
                            kotv = kot[:].rearrange("p (r q) d -> p r q d", q=2)
                            for hp in range(2):
                                TT("dve", sTv[:, :, hp, :], bks[hp][0][:, 0:256].rearrange("p (r l) -> p r l", l=128),
                                   mk[:].unsqueeze(1).to_broadcast([128, 2, 128]), ALU.mult, [r_mk], [bks[hp][1], r_sT])
                                CP("act", kotv[:, :, hp, :], bkts[hp][0][:].bitcast(BF16)[:, 0:128].rearrange("p (r d) -> p r d", d=64),
                                   [], [bkts[hp][1], r_kot])
                            yield
                            bko, rbo = nbank("aux")
                            for h_ in range(4):
                                pr, hp = h_ // 2, h_ % 2
                                ps_ = slice(hp * 64, (hp + 1) * 64)
                                MM(bko[:, h_ * 128:(h_ + 1) * 128], sT[:, h_, :], vch[:, ch, h_ * 128:(h_ + 1) * 128], True, False,
                                   [r_sT, r_vch], [rbo])
                                MM(bko[:, h_ * 128:(h_ + 1) * 128], prod[ps_, 2, pr, c0:c0 + 128], stb[ps_, pr, :], False, True,
                                   [r_prod, r_stb], [rbo])
                            bku, rbu = nbank("tr")
                            for h_ in range(4):
                                pr, hp = h_ // 2, h_ % 2
                                MM(bku[hp * 64:(hp + 1) * 64, pr * 128:(pr + 1) * 128], kot[:, h_, :], vch[:, ch, h_ * 128:(h_ + 1) * 128],
                                   True, True, [r_kot, r_vch], [rbu])
                            TT("pool", stt[:], stt[:], dec[:, :, ch].unsqueeze(2).to_broadcast([128, 2, 128]), ALU.mult,
                               [r_dec], [r_stt])
                            yield
                            TT("dve", stt[:], stt[:], bku[:, 0:256].rearrange("p (r v) -> p r v", v=128), ALU.add, [], [rbu, r_stt])
                            CP("act" if dn == 0 else "dve", obuf[:, ch, :], bko[:], [], [rbo, r_ob])
                            yield
                            CP("act", stb[:], stt[:], [r_stt], [r_stb])
                            yield
                        DMA("sp", o_dst[tok0:tok0 + ntok, :].rearrange("(s p) f -> p s f", p=128), obuf[:], [r_ob], [])

                    def seq():
                        for _ in prep(0):
                            yield
                        for gi_ in range(len(order)):
                            c_ = chunks(gi_)
                            p_ = prep(gi_ + 1) if gi_ + 1 < len(order) else None
                            while c_ is not None or p_ is not None:
                                if c_ is not None:
                                    try:
                                        next(c_)
                                    except StopIteration:
                                        c_ = None
                                if p_ is not None:
                                    try:
                                        next(p_)
                                    except StopIteration:
                                        p_ = None
                                yield
                    return seq()

                for _ in itertools.zip_longest(gla_dir(0), gla_dir(1)):
                    pass
                S.flush()

            stW = contextlib.ExitStack()
            wbr, r_wbr = sb(stW, "wbr", [128, 12, D], BF16)
            wo, r_wo = sb(stW, "wo", [128, 8, D], BF16)
            wgs, r_wgs = sb(stW, "wgs", [128, 8, 3072], BF16)

            mw_pieces = []
            for k6 in range(6):
                mw_pieces.append((wgs[:, :, k6 * 512:(k6 + 1) * 512],
                                  w_in[l][:, 3360 + k6 * 512:3360 + (k6 + 1) * 512].rearrange("(kc p) n -> p kc n", p=128), r_wgs))
            for k in range(3):
                mw_pieces.append((wbr[:, k * 4:(k + 1) * 4, :], w_br[l][k].rearrange("(wc p) n -> p wc n", p=128), r_wbr))
            for k2 in range(2):
                mw_pieces.append((wo[:, :, k2 * 512:(k2 + 1) * 512],
                                  w_o[l][:, k2 * 512:(k2 + 1) * 512].rearrange("(jc p) n -> p jc n", p=128), r_wo))

            def issue_merge_piece():
                if mw_pieces:
                    o_, i_, r_ = mw_pieces.pop(0)
                    DMA("pool", o_, i_, [], [r_])

            with contextlib.ExitStack() as st:
                kTa, r_kTa = sb(st, "kTa", [64, 2, NTOK], BF16)
                Vxa, r_Vxa = sb(st, "Vxa", [128, 34, 130], BF16)
                sk32, r_sk32 = sb(st, "sk32", [128, 8], F32)
                esk, r_esk = sb(st, "esk", [128, 8], F32)
                DMA("sp", kTa[:], kT_s.rearrange("(h d) t -> d h t", d=64), [], [r_kTa])
                DMA("sp", Vxa[:], Vx_s.rearrange("(s p) f -> p s f", p=128), [], [r_Vxa])
                DMA("sp", sk32[:], sink_d[l], [], [r_sk32])
                ACTF(esk[:], sk32[:], AF.Exp, [r_sk32], [r_esk])
                qTt = Rot(nc, st, "b_qT", [64, 8, 512], BF16, n=3)
                pts = Rot(nc, st, "b_pt", [128, 512], BF16, n=24)
                dens = Rot(nc, st, "b_den", [128, 8], F32, n=8)
                bos = Rot(nc, st, "b_bo", [128, 512], BF16, n=4)
                boTs = Rot(nc, st, "b_boT", [128, 4, 512], BF16, n=3)

                sgroups = [(t0_, n_) for (t0_, n_) in TG if not (last and t0_ == 0)]
                swa_next = {sgroups[i_][0]: sgroups[i_ + 1] for i_ in range(len(sgroups) - 1)}
                qT_pref = {}

                def swa_qload(tok0, ntok):
                    qT_, r_qT_ = qTt.next()
                    DMA("sp", qT_[:, :, 0:ntok], qT_s[:, tok0:tok0 + ntok].rearrange("(h d) t -> d h t", d=64), [], [r_qT_])
                    qT_pref[tok0] = (qT_, r_qT_)

                def swa_tile(tok0, ntok, sub, grp):
                    nsub = ntok // 128
                    if sub == 0:
                        if tok0 not in qT_pref:
                            swa_qload(tok0, ntok)
                        grp["qT"], grp["r_qT"] = qT_pref.pop(tok0)
                        nx_ = swa_next.get(tok0)
                        if nx_ is not None:
                            swa_qload(*nx_)
                        grp["boT"], _, grp["regs"] = boTs.next2()
                        grp["left"] = nsub
                    qT, r_qT, boT = grp["qT"], grp["r_qT"], grp["boT"]
                    i = tok0 // 128 + sub
                    if i < 2:
                        kbs = [(0, None), (1, None)]
                    else:
                        kbs = []
                        if i - 1 >= 2:
                            kbs.append((i - 1, "ge"))
                        kbs.append((i, None))
                        if i + 1 <= 33:
                            kbs.append((i + 1, "le"))
                        kbs += [(0, None), (1, None)]
                    bo, r_bo = bos.next()

                    def A(kh):
                        plist = []
                        for (kb, m) in kbs:
                            bk, rb = nbank("mm")
                            MM(bk[:], kTa[:, kh, kb * 128:(kb + 1) * 128], qT[:, 4 * kh:4 * kh + 4, sub * 128:(sub + 1) * 128],
                               True, True, [r_kTa, r_qT], [rb])
                            pt, r_pt = pts.next()
                            ACTF(pt[:], bk[:], AF.Exp, [], [rb, r_pt], scale=0.125)
                            if m is not None:
                                mk, r_mk = (mge, r_mge) if m == "ge" else (mle, r_mle)
                                TT("pool", pt[:].rearrange("p (h q) -> p h q", q=128), pt[:].rearrange("p (h q) -> p h q", q=128),
                                   mk[:].unsqueeze(1).to_broadcast([128, 4, 128]), ALU.mult, [r_mk], [r_pt])
                            plist.append((pt, r_pt, kb))
                        return plist

                    def B(kh, plist):
                        bko, rbo = nbank("aux")
                        for hh in range(4):
                            for idx, (pt, r_pt, kb) in enumerate(plist):
                                MM(bko[:, hh * 128:hh * 128 + 65], pt[:, hh * 128:(hh + 1) * 128], Vxa[:, kb, kh * 65:(kh + 1) * 65],
                                   idx == 0, idx == len(plist) - 1, [r_pt, r_Vxa], [rbo])
                        return bko, rbo

                    def C(kh, bko, rbo):
                        den, r_den = dens.next()
                        pov = bko[:].rearrange("p (h d) -> p h d", d=128)
                        TT("dve", den[:, 0:4], pov[:, :, 64], esk[:, 4 * kh:4 * kh + 4], ALU.add, [r_esk], [rbo, r_den])
                        RCP(den[:, 4:8], den[:, 0:4], [r_den], [r_den])
                        TT("dve", bo[:, kh * 256:(kh + 1) * 256].rearrange("p (h d) -> p h d", d=64), pov[:, :, 0:64],
                           den[:, 4:8].unsqueeze(2).to_broadcast([128, 4, 64]), ALU.mult, [r_den], [rbo, r_bo])

                    pl0 = A(0)
                    yield
                    pl1 = A(1)
                    po0 = B(0, pl0)
                    yield
                    C(0, *po0)
                    po1 = B(1, pl1)
                    yield
                    C(1, *po1)
                    yield
                    bk2, rb2 = nbank("tr")
                    bv = bk2[:].bitcast(BF16)
                    for c in range(4):
                        TR(bv[:, c * 128:(c + 1) * 128], bo[:, c * 128:(c + 1) * 128], identb[:], [r_bo, r_identb], [rb2])
                    yield
                    CP("act", boT[:, :, sub * 128:(sub + 1) * 128], bv[:, 0:512].rearrange("p (c t) -> p c t", t=128),
                       [], [rb2, grp["regs"][sub]])
                    grp["left"] -= 1
                    if grp["left"] == 0:
                        DMA("sp", brT[GI[tok0], :, 4:8, 0:ntok], boT[:, :, 0:ntok],
                            grp["regs"][0:nsub], [])

                def swa_all():
                    nt_ = 0
                    for (tok0, ntok) in sgroups:
                        grp = {}
                        for sub in range(ntok // 128):
                            yield swa_tile(tok0, ntok, sub, grp)
                            nt_ += 1
                            if nt_ >= 3 and nt_ % 2 == 1:
                                issue_merge_piece()
                    while mw_pieces:
                        issue_merge_piece()
                pipe(swa_all(), 3)
                S.flush()

            with contextlib.ExitStack() as st:
                ND = 5
                wk = norm_wk(st, "c3_", n=ND)
                ofs = Rot(nc, st, "c3_of", [128, 4, 512], F32, n=3)
                obs = Rot(nc, st, "c3_ob", [128, 4, 512], F32, n=3)
                rss = Rot(nc, st, "c3_rs", [128, 4, 512], BF16, n=3)
                os_ = Rot(nc, st, "c3_o", [128, 512], F32, n=ND)
                cbs = Rot(nc, st, "c3_cb", [128, 512], BF16, n=3)
                cTs = Rot(nc, st, "c3_cT", [128, 4, 512], BF16, n=2)

                c3groups = [(t0_, n_) for (t0_, n_) in TG if not (last and t0_ == 0)]
                c3_next = {c3groups[i_][0]: c3groups[i_ + 1] for i_ in range(len(c3groups) - 1)}
                c3_pref = {}

                def c3_load(tok0, ntok):
                    nsub = ntok // 128
                    of_, r_of_ = ofs.next()
                    ob_, r_ob_ = obs.next()
                    rs_, r_rs_ = rss.next()
                    DMA("sp", of_[:, 0:nsub, :], of_s[tok0:tok0 + ntok, :].rearrange("(s p) f -> p s f", p=128), [], [r_of_])
                    DMA("sp", ob_[:, 0:nsub, :], ob_s[tok0:tok0 + ntok, :].rearrange("(s p) f -> p s f", p=128), [], [r_ob_])
                    DMA("sp", rs_[:, 0:nsub, :], rs_s[tok0:tok0 + ntok, :].rearrange("(s p) f -> p s f", p=128), [], [r_rs_])
                    c3_pref[tok0] = (of_, r_of_, ob_, r_ob_, rs_, r_rs_)

                def c3_sub(tok0, ntok, sub, grp):
                    nsub = ntok // 128
                    if sub == 0:
                        if tok0 not in c3_pref:
                            c3_load(tok0, ntok)
                        (grp["of"], grp["r_of"], grp["ob"], grp["r_ob"], grp["rs"], grp["r_rs"]) = c3_pref.pop(tok0)
                        nx_ = c3_next.get(tok0)
                        if nx_ is not None:
                            c3_load(*nx_)
                        grp["cT"], _, grp["regs"] = cTs.next2()
                        grp["left"] = nsub
                    o_, r_o = os_.next()
                    TT("dve", o_[:], grp["of"][:, sub, :], grp["ob"][:, sub, :], ALU.add, [grp["r_of"], grp["r_ob"]], [r_o])
                    yield
                    res = None
                    for res in rstd_g(wk, o_[:], r_o, 4, 128, True):
                        yield
                    st4, r_st4 = res
                    TT("dve", o_[:].rearrange("p (h d) -> p h d", d=128), o_[:].rearrange("p (h d) -> p h d", d=128),
                       st4[:, 16:20].unsqueeze(2).to_broadcast([128, 4, 128]), ALU.mult, [r_st4], [r_o])
                    yield
                    cb, r_cb = cbs.next()
                    TT("pool", cb[:], o_[:], grp["rs"][:, sub, :], ALU.mult, [r_o, grp["r_rs"]], [r_cb])
                    yield
                    bk2, rb2 = nbank("tr")
                    bv = bk2[:].bitcast(BF16)
                    for c in range(4):
                        TR(bv[:, c * 128:(c + 1) * 128], cb[:, c * 128:(c + 1) * 128], identb[:], [r_cb, r_identb], [rb2])
                    yield
                    CP("act", grp["cT"][:, :, sub * 128:(sub + 1) * 128], bv[:, 0:512].rearrange("p (c t) -> p c t", t=128),
                       [], [rb2, grp["regs"][sub]])
                    grp["left"] -= 1
                    if grp["left"] == 0:
                        DMA("sp", brT[GI[tok0], :, 8:12, 0:ntok], grp["cT"][:, :, 0:ntok],
                            grp["regs"][0:nsub], [])

                def c3_all():
                    for (tok0, ntok) in TG:
                        if last and tok0 == 0:
                            continue
                        grp = {}
                        for sub in range(ntok // 128):
                            yield c3_sub(tok0, ntok, sub, grp)
                pipe(c3_all(), ND)
                S.flush()

            with contextlib.ExitStack() as st:
                brts = Rot(nc, st, "m_br", [128, 12, 512], BF16)
                xts = Rot(nc, st, "m_xt", [128, 4, D], F32)
                junk, r_junk = sb(st, "m_junk", [128, D], BF16)
                sss = Rot(nc, st, "m_ss", [128, 12], F32, n=2)
                xn, r_xn = sb(st, "m_xn", [128, 4, D], BF16)
                hTm, r_hTm = sb(st, "m_hT", [128, 8, 512], BF16)
                sgb = Rot(nc, st, "m_sgb", [128, 512], BF16, n=4)
                tms = Rot(nc, st, "m_tm", [128, 512], F32, n=4)
                yT, r_yT = sb(st, "m_yT", [128, 8, 512], BF16)
                mgroups = [(t0_, n_) for (t0_, n_) in TG if not (last and t0_ == 0)]

                def m_load(tok0, ntok):
                    brt, r_brt = brts.next()
                    xt, r_xt = xts.next()
                    ss, r_ss = sss.next()
                    DMA("sp", brt[:, :, 0:ntok], brT[GI[tok0], :, :, 0:ntok], [], [r_brt])
                    return (brt, r_brt, xt, r_xt, ss, r_ss)

                wo_scaled = [False]
                glcol, r_glcol = sb(st, "glcol", [128, 1], F32)
                DMA("sp", glcol[:], glcol_d[l], [], [r_glcol])
                for wc in range(4):
                    TS("dve" if wc % 2 else "pool", wbr[:, 8 + wc, :], wbr[:, 8 + wc, :], glcol[:, 0:1], ALU.mult, [r_glcol], [r_wbr])
                ld = m_load(*mgroups[0])
                bufsA = (junk, Region("junk"), ld[4], ld[5], xn, r_xn)
                frontA(bufsA, src_x, mgroups[0][0], mgroups[0][1], ld[2], ld[3])
                for mi, (tok0, ntok) in enumerate(mgroups):
                    nsub = ntok // 128
                    s = 1 if tok0 == 0 else 0
                    brt, r_brt, xt, r_xt, ss, r_ss = ld
                    frontB((junk, None, ss, r_ss, xn, r_xn), ntok, 0, 8, s, hTm[:, :, 0:ntok], r_hTm)
                    if mi + 1 < len(mgroups):
                        ld = m_load(*mgroups[mi + 1])
                    if s == 0 and not wo_scaled[0]:
                        for jc in range(8):
                            TT("pool" if jc % 2 else "dve", wo[:, jc, :], wo[:, jc, :], gbc[:, 0, 0, :], ALU.mult, [r_gbc], [r_wo])
                        wo_scaled[0] = True
                    for j in range(8):
                        tks = []
                        for k in range(3):
                            bkg, rbg = nbank("mm")
                            c0 = k * 1024 + j * 128
                            for kc in range(8):
                                MM(bkg[:, 0:ntok], wgs[:, kc, c0:c0 + 128], hTm[:, kc, 0:ntok], kc == 0, kc == 7, [r_wgs, r_hTm], [rbg])
                            sg, r_sg = sgb.next()
                            ACTF(sg[:, 0:ntok], bkg[:, 0:ntok], AF.Sigmoid, [], [rbg, r_sg])
                            bk, rb = nbank("aux")
                            for wc in range(4):
                                MM(bk[:, 0:ntok], wbr[:, k * 4 + wc, j * 128:(j + 1) * 128], brt[:, k * 4 + wc, 0:ntok], wc == 0, wc == 3,
                                   [r_wbr, r_brt], [rb])
                            tm, r_tm = tms.next()
                            TT("dve", tm[:, 0:ntok], bk[:, 0:ntok], sg[:, 0:ntok], ALU.mult, [r_sg], [rb, r_tm])
                            tks.append((tm, r_tm))
                        TT("pool", tks[0][0][:, 0:ntok], tks[0][0][:, 0:ntok], tks[1][0][:, 0:ntok], ALU.add, [tks[1][1]], [tks[0][1]])
                        TT("pool", yT[:, j, 0:ntok], tks[0][0][:, 0:ntok], tks[2][0][:, 0:ntok], ALU.add, [tks[0][1], tks[2][1]], [r_yT])
                    if mi + 1 < len(mgroups):
                        frontA((junk, None, ld[4], ld[5], xn, r_xn), src_x, mgroups[mi + 1][0], mgroups[mi + 1][1], ld[2], ld[3])
                    for sub in range(nsub):
                        for half in range(2):
                            bk, rb = nbank("tr")
                            for jc in range(8):
                                MM(bk[:], yT[:, jc, sub * 128:(sub + 1) * 128], wo[:, jc, half * 512:(half + 1) * 512], jc == 0, jc == 7,
                                   [r_yT, r_wo], [rb])
                            xs_ = xt[:, sub, half * 512:(half + 1) * 512]
                            if s == 1:
                                tm, r_tm = tms.next()
                                TT("dve", tm[:], bk[:], gbc[:, 0, s, half * 512:(half + 1) * 512], ALU.mult, [r_gbc], [rb, r_tm])
                                TT("pool", xs_, tm[:], xs_, ALU.add, [r_tm], [r_xt])
                            else:
                                TT("dve", xs_, bk[:], xs_, ALU.add, [], [rb, r_xt])
                    DMA("sp", xm[tok0:tok0 + ntok, :].rearrange("(s p) d -> p s d", p=128), xt[:, 0:nsub, :], [r_xt], [])
                S.flush()

            stW.close()

            with contextlib.ExitStack() as st:
                w1, r_w1 = sb(st, "w1", [128, 8, 4096], BF16)
                w2, r_w2 = sb(st, "w2", [128, 32, D], BF16)
                r_w1h = [Region("w1h%d" % i_) for i_ in range(2)]
                r_w2h = [Region("w2h%d" % i_) for i_ in range(2)]
                for hf in range(2):
                    DMA("pool", w1[:, :, hf * 2048:(hf + 1) * 2048],
                        w_ff1[l][:, hf * 2048:(hf + 1) * 2048].rearrange("(kc p) n -> p kc n", p=128), [], [r_w1h[hf]])
                for hf in range(2):
                    DMA("pool", w2[:, hf * 16:(hf + 1) * 16, :],
                        w_ff2[l][hf * 2048:(hf + 1) * 2048, :].rearrange("(fc p) n -> p fc n", p=128), [], [r_w2h[hf]])
                junk, r_junk = sb(st, "F_junk", [128, D], BF16)
                sss = Rot(nc, st, "F_ss", [128, 12], F32, n=2)
                xns = Rot(nc, st, "F_xn", [128, 2, D], BF16, n=2)
                xts = Rot(nc, st, "F_xt", [128, 2, D], F32, n=2)
                h2s = Rot(nc, st, "F_h2", [128, 8, 256], BF16, n=2)
                f1, r_f1 = sb(st, "F_f1", [128, 32, 256], BF16)
                rts = Rot(nc, st, "F_rt", [128, 256], F32, n=3)
                tms = Rot(nc, st, "F_tm", [128, 512], F32, n=2)
                fgroups = [(t0_, n_) for (t0_, n_) in TG256 if not (last and t0_ == 0)]

                def F_A(tok0, ntok):
                    ctx_ = {"tok0": tok0, "ntok": ntok, "s": 1 if tok0 == 0 else 0}
                    ctx_["xt"], ctx_["r_xt"] = xts.next()
                    ss, r_ss = sss.next()
                    xn, r_xn = xns.next()
                    ctx_["bufs"] = (junk, Region("junk"), ss, r_ss, xn, r_xn)
                    frontA(ctx_["bufs"], xm, tok0, ntok, ctx_["xt"], ctx_["r_xt"])
                    return ctx_

                def F_B(c_):
                    c_["h2"], c_["r_h2"] = h2s.next()
                    frontB(c_["bufs"], c_["ntok"], 24, 32, c_["s"], c_["h2"][:, :, 0:c_["ntok"]], c_["r_h2"])

                def F_ff1(c_):
                    ntok, h2, r_h2 = c_["ntok"], c_["h2"], c_["r_h2"]
                    for fc in range(32):
                        bk, rb = nbank("mm")
                        for kc in range(8):
                            MM(bk[:, 0:ntok], w1[:, kc, fc * 128:(fc + 1) * 128], h2[:, kc, 0:ntok], kc == 0, kc == 7, [r_w1h[fc // 16], r_h2], [rb])
                        rt, r_rt = rts.next()
                        ACTF(rt[:, 0:ntok], bk[:, 0:ntok], AF.Relu, [], [rb, r_rt])
                        TT("dve" if fc % 2 else "pool", f1[:, fc, 0:ntok], rt[:, 0:ntok], rt[:, 0:ntok], ALU.mult, [r_rt], [r_f1])

                def F_ff2(c_):
                    tok0, ntok, s, xt, r_xt = c_["tok0"], c_["ntok"], c_["s"], c_["xt"], c_["r_xt"]
                    for sub in range(2):
                        for half in range(2):
                            bk, rb = nbank("aux")
                            for fc in range(32):
                                MM(bk[:], f1[:, fc, sub * 128:(sub + 1) * 128], w2[:, fc, half * 512:(half + 1) * 512], fc == 0, fc == 31,
                                   [r_f1, r_w2h[fc // 16]], [rb])
                            tm, r_tm = tms.next()
                            TT("dve", tm[:], bk[:], gbc[:, 1, s, half * 512:(half + 1) * 512], ALU.mult, [r_gbc], [rb, r_tm])
                            TT("pool", xt[:, sub, half * 512:(half + 1) * 512], tm[:], xt[:, sub, half * 512:(half + 1) * 512], ALU.add,
                               [r_tm], [r_xt])
                    if last:
                        DMA("sp", out[tok0 - NCTX:tok0 - NCTX + ntok, :].rearrange("(s p) d -> p s d", p=128), xt[:], [r_xt], [])
                    else:
                        DMA("sp", xa[tok0:tok0 + ntok, :].rearrange("(s p) d -> p s d", p=128), xt[:], [r_xt], [])

                cur = F_A(*fgroups[0])
                F_B(cur)
                for gi_ in range(len(fgroups)):
                    F_ff1(cur)
                    nxt_c = F_A(*fgroups[gi_ + 1]) if gi_ + 1 < len(fgroups) else None
                    F_ff2(cur)
                    if nxt_c is not None:
                        F_B(nxt_c)
                    cur = nxt_c
                S.flush()
    return nc


def _rope_tables():
    n_freq = 16
    freqs = (10000.0 ** (-np.arange(n_freq, dtype=np.float32) / n_freq)).astype(np.float32)
    t = np.arange(NLAT)
    row = (t // 64).astype(np.float32)
    col = (t % 64).astype(np.float32)
    ar = row[:, None] * freqs[None, :]
    ac = col[:, None] * freqs[None, :]
    C = np.concatenate([np.cos(ar), np.cos(ar), np.cos(ac), np.cos(ac)], axis=1).astype(np.float32)
    Sg = np.concatenate([-np.sin(ar), np.sin(ar), -np.sin(ac), np.sin(ac)], axis=1).astype(np.float32)
    Call = np.concatenate([np.ones((NCTX, 64), np.float32), C], axis=0)
    Sall = np.concatenate([np.zeros((NCTX, 64), np.float32), Sg], axis=0)
    Ct = np.ascontiguousarray(Call.reshape(34, 128, 64).transpose(1, 0, 2))
    St = np.ascontiguousarray(Sall.reshape(34, 128, 64).transpose(1, 0, 2))
    return Ct, St


def _prep(inputs):
    f = lambda a: np.ascontiguousarray(np.asarray(a, dtype=np.float32))
    x, c, ctx, c_ctx = f(inputs["x"]), f(inputs["c"]), f(inputs["ctx"]), f(inputs["c_ctx"])
    B = x.shape[0]
    bc = lambda a, p=128: np.ascontiguousarray(np.broadcast_to(a, (DEPTH, p) + a.shape[2:]))
    b_ada = f(inputs["b_ada"])
    q_norm, k_norm = f(inputs["q_norm"]), f(inputs["k_norm"])
    shared = {
        "w_ada": f(inputs["w_ada"]),
        "badaT": np.ascontiguousarray(b_ada.reshape(DEPTH, 48, 128).transpose(0, 2, 1)),
        "badaR": np.ascontiguousarray(np.broadcast_to(b_ada[:, None, :], (DEPTH, 2, 6 * D))),
        "w_in": f(inputs["w_in"]),
        "qg64": bc(np.stack([q_norm, q_norm.reshape(DEPTH, 2, 2, 16)[:, :, ::-1, :].reshape(DEPTH, 64)], axis=1)[:, None, :, :]),
        "kg64": bc(np.stack([k_norm, k_norm.reshape(DEPTH, 2, 2, 16)[:, :, ::-1, :].reshape(DEPTH, 64)], axis=1)[:, None, :, :]),
        "sinkb": bc(f(inputs["sink"])[:, None, :]),
        "sgcol": np.ascontiguousarray(f(inputs["sgu_norm"]).reshape(DEPTH, 4, 128).transpose(0, 2, 1)),
        "wsguT": np.ascontiguousarray(f(inputs["w_sgu"]).transpose(0, 3, 1, 2)),
        "bsgub": bc(f(inputs["b_sgu"])[:, None, :, :]),
        "wgate": np.ascontiguousarray(np.stack([f(inputs["w_gate_f"]), f(inputs["w_gate_b"])], axis=1)),
        "bgrow": np.ascontiguousarray(np.stack([f(inputs["b_gate_f"]), f(inputs["b_gate_b"])], axis=1)[:, None, :, :]),
        "glcol": np.ascontiguousarray(f(inputs["gla_norm"])[:, :, None]),
        "w_br": f(inputs["w_br"]), "w_o": f(inputs["w_o"]), "w_ff1": f(inputs["w_ff1"]), "w_ff2": f(inputs["w_ff2"]),
        "ident": np.eye(128, dtype=np.float32),
        "mask_ge": (np.arange(128)[:, None] >= np.arange(128)[None, :]).astype(np.float32),
        "mask_le": (np.arange(128)[:, None] <= np.arange(128)[None, :]).astype(np.float32),
        "sel": np.ascontiguousarray(np.broadcast_to(np.eye(2, dtype=np.float32)[:, :, None], (2, 2, 128)).transpose(1, 0, 2)),
    }
    shared["ropeC"], shared["ropeS"] = _rope_tables()
    shared["sel"] = np.ascontiguousarray(np.broadcast_to(np.eye(2, dtype=np.float32)[:, :, None], (2, 2, 128)))
    in_maps = []
    for b in range(B):
        m = dict(shared)
        m["xcat"] = np.ascontiguousarray(np.concatenate([ctx[b], x[b]], axis=0))
        cm = np.stack([c[b].reshape(8, 128).T, c_ctx.reshape(8, 128).T], axis=-1)
        m["cmod"] = np.ascontiguousarray(cm)
        in_maps.append(m)
    return in_maps


_NC_CACHE = {}


def kernel(**inputs):
    in_maps = _prep(inputs)
    if "nc" not in _NC_CACHE:
        _NC_CACHE["nc"] = build()
    nc = _NC_CACHE["nc"]
    res = run_bass_kernel_spmd(nc, in_maps, core_ids=list(range(len(in_maps))))
    return np.stack([np.asarray(r["out"], dtype=np.float32) for r in res.results], axis=0)
```
